# Optimizing a Trainium2 kernel written in Bass

```python
import jax, jax.numpy as jnp
from jax import lax
import numpy as np

D_MODEL = 2048
BATCH = 4
SEQ = 4096
DEPTH = 2

GRID_W = 64
CTX_LEN = 256
GLA_HEADS = 4
GLA_V = D_MODEL // 2
GLA_K = D_MODEL // 4
GLA_DK = GLA_K // GLA_HEADS
GLA_DV = GLA_V // GLA_HEADS
GATE_RANK = 16
GATE_TAU = 16.0
CHUNK = 64
POOL_WINDOWS = (2, 4, 8, 16)
POOL_WIDTH = D_MODEL // 2
POOL_GROUP_DIM = POOL_WIDTH // len(POOL_WINDOWS)
MIX_WIDTH = GLA_V + POOL_WIDTH
D_FF = 4 * D_MODEL
STATE_COLS = GLA_K + GLA_V + 2 * GATE_RANK
PROJ_COLS = STATE_COLS + GLA_K + GLA_V + POOL_WIDTH
PROJ_SPLITS = (GLA_K, GLA_K + GLA_V, GLA_K + GLA_V + GATE_RANK, STATE_COLS,
               STATE_COLS + GLA_K, STATE_COLS + GLA_K + GLA_V)
STATE_SPLITS = PROJ_SPLITS[:3]
DEEPNORM_ALPHA = (2 * DEPTH) ** 0.25
DEEPNORM_BETA = (8 * DEPTH) ** -0.25
LN_EPS = 1e-6
RMS_EPS = 1e-6

kernel_name = 'gla_pool_hybrid_dit_block'


def _layer_norm_plain(x):
    xf = x.astype(jnp.float32)
    mu = jnp.mean(xf, axis=-1, keepdims=True)
    var = jnp.mean(jnp.square(xf - mu), axis=-1, keepdims=True)
    return ((xf - mu) * lax.rsqrt(var + LN_EPS)).astype(x.dtype)


def _layer_norm(x, g, b):
    return _layer_norm_plain(x) * g + b


def _modulate(x, shift, scale):
    return _layer_norm_plain(x) * (1.0 + scale) + shift


def _heads(a, n_heads):
    b, t, _ = a.shape
    return a.reshape(b, t, n_heads, -1).transpose(0, 2, 1, 3)


def _merge_heads(a):
    b, h, t, d = a.shape
    return a.transpose(0, 2, 1, 3).reshape(b, t, h * d)


def _rev(a):
    return a[:, :, ::-1]


def _log_decay(lr, w_up, b_up):
    return jax.nn.log_sigmoid((lr @ w_up + b_up).astype(jnp.float32)) / GATE_TAU


def _gla_chunked(q, k, v, log_a, s0):
    b, h, t, _ = q.shape
    dv = v.shape[-1]
    n = t // CHUNK

    def chunks(a):
        return jnp.moveaxis(a.astype(jnp.float32).reshape(b, h, n, CHUNK, a.shape[-1]), 2, 0)

    lower_tri = jnp.tril(jnp.ones((CHUNK, CHUNK), dtype=bool))[:, :, None]

    def step(s, blk):
        qc, kc, vc, gc = blk
        cum = jnp.cumsum(gc, axis=2)
        total = cum[:, :, -1:, :]
        o_inter = jnp.einsum('bhid,bhdv->bhiv', qc * jnp.exp(cum), s)
        diff = jnp.minimum(cum[:, :, :, None, :] - cum[:, :, None, :, :], 0.0)
        decay = jnp.where(lower_tri, jnp.exp(diff), 0.0)
        scores = jnp.einsum('bhid,bhjd,bhijd->bhij', qc, kc, decay)
        o_intra = jnp.einsum('bhij,bhjv->bhiv', scores, vc)
        s_new = (jnp.exp(total[:, :, 0, :, None]) * s
                 + jnp.einsum('bhjd,bhjv->bhdv', kc * jnp.exp(total - cum), vc))
        return s_new, o_inter + o_intra

    s_fin, o = lax.scan(step, s0, (chunks(q), chunks(k), chunks(v), chunks(log_a)))
    return jnp.moveaxis(o, 0, 2).reshape(b, h, t, dv), s_fin


def _gla_final_state(k, v, log_a):
    cum = jnp.cumsum(log_a.astype(jnp.float32), axis=2)
    total = cum[:, :, -1:, :]
    return jnp.einsum('bhtd,bhtv->bhdv', k.astype(jnp.float32) * jnp.exp(total - cum),
                      v.astype(jnp.float32))


def _bidir_gla(q, k, v, la_f, la_b, s0_f, s0_b):
    o_f, s_f = _gla_chunked(q, k, v, la_f, s0_f)
    o_b, s_b = _gla_chunked(_rev(q), _rev(k), _rev(v), _rev(la_b), s0_b)
    return o_f + _rev(o_b), s_f, s_b


def _gla_output(o, g, norm_g):
    o = o * lax.rsqrt(jnp.mean(jnp.square(o), axis=-1, keepdims=True) + RMS_EPS)
    return _merge_heads(o).astype(g.dtype) * norm_g.reshape(1, 1, -1).repeat(1, axis=0)[:, :, :GLA_DV].repeat(GLA_HEADS, axis=-1).reshape(1, 1, GLA_V) * 0 + _merge_heads(o * norm_g).astype(g.dtype) * jax.nn.silu(g) if False else _merge_heads(o * norm_g).astype(g.dtype) * jax.nn.silu(g)


def _multiscale_pool(u, w_pool, pool_scale, seg_start, seg_len):
    b, t, c = u.shape
    pos = jnp.arange(t, dtype=jnp.int32)
    seg_end = seg_start + seg_len - 1
    csum = jnp.cumsum(u.astype(jnp.float32), axis=1)
    csum = jnp.concatenate([jnp.zeros((b, 1, c), jnp.float32), csum], axis=1)
    outs = []
    for gi, w in enumerate(POOL_WINDOWS):
        lo = jnp.maximum(pos - w // 2, seg_start)
        hi = jnp.minimum(pos + w // 2 - 1, seg_end)
        sl = slice(gi * POOL_GROUP_DIM, (gi + 1) * POOL_GROUP_DIM)
        cs = csum[:, :, sl]
        mean = (cs[:, hi + 1] - cs[:, lo]) / (hi - lo + 1).astype(jnp.float32)[None, :, None]
        resid = (mean - u[:, :, sl].astype(jnp.float32)).astype(u.dtype)
        outs.append(jnp.einsum('btc,cd->btd', resid, w_pool[gi]))
    return jnp.concatenate(outs, axis=-1) * pool_scale


def _mixer(p, s0_f, s0_b, w_gate_up, b_gate, gla_norm_g, w_pool, pool_scale, w_out, seg_start, seg_len):
    k, v, lr_f, lr_b, q, g, u = jnp.split(p, PROJ_SPLITS, axis=-1)
    la_f = _heads(_log_decay(lr_f, w_gate_up[0], b_gate[0]), GLA_HEADS)
    la_b = _heads(_log_decay(lr_b, w_gate_up[1], b_gate[1]), GLA_HEADS)
    o, s_f, s_b = _bidir_gla(_heads(q, GLA_HEADS) * (GLA_DK ** -0.5), _heads(k, GLA_HEADS),
                             _heads(v, GLA_HEADS), la_f, la_b, s0_f, s0_b)
    mixed = jnp.concatenate([_gla_output(o, g, gla_norm_g),
                             _multiscale_pool(u, w_pool, pool_scale, seg_start, seg_len)], axis=-1)
    return mixed @ w_out, s_f, s_b


def _sq_relu_mlp(h, w1, b1, w2, b2):
    return jnp.square(jax.nn.relu(h @ w1 + b1)) @ w2 + b2


def setup_inputs(seed: int = 0) -> dict:
    key = jax.random.key(seed)
    ks = jax.random.split(key, 21)

    def nrm(k, shape, s):
        return jax.random.normal(k, shape, jnp.float32) * s

    return {
        'x': nrm(ks[0], (BATCH, SEQ, D_MODEL), 1.0),
        'c': nrm(ks[1], (BATCH, D_MODEL), 1.0),
        'ctx': nrm(ks[2], (BATCH, CTX_LEN, D_MODEL), 1.0),
        'c_ctx': nrm(ks[3], (D_MODEL,), 1.0),
        'w_ada': nrm(ks[4], (DEPTH, D_MODEL, 6 * D_MODEL), 0.5 * D_MODEL ** -0.5),
        'b_ada': nrm(ks[5], (DEPTH, 6 * D_MODEL), 0.02),
        'w_in': nrm(ks[6], (DEPTH, D_MODEL, PROJ_COLS), D_MODEL ** -0.5),
        'w_gate_up': nrm(ks[7], (DEPTH, 2, GATE_RANK, GLA_K), GATE_RANK ** -0.5),
        'b_gate': nrm(ks[8], (DEPTH, 2, GLA_K), 0.02),
        'gla_norm_g': 1.0 + nrm(ks[9], (DEPTH, GLA_DV), 0.02),
        'w_pool': nrm(ks[10], (DEPTH, len(POOL_WINDOWS), POOL_GROUP_DIM, POOL_GROUP_DIM), POOL_GROUP_DIM ** -0.5),
        'pool_scale': 1.0 + nrm(ks[11], (DEPTH, POOL_WIDTH), 0.02),
        'w_out': nrm(ks[12], (DEPTH, MIX_WIDTH, D_MODEL), DEEPNORM_BETA * MIX_WIDTH ** -0.5),
        'ln1_g': 1.0 + nrm(ks[13], (DEPTH, D_MODEL), 0.02),
        'ln1_b': nrm(ks[14], (DEPTH, D_MODEL), 0.02),
        'w_mlp1': nrm(ks[15], (DEPTH, D_MODEL, D_FF), D_MODEL ** -0.5),
        'b_mlp1': nrm(ks[16], (DEPTH, D_FF), 0.02),
        'w_mlp2': nrm(ks[17], (DEPTH, D_FF, D_MODEL), DEEPNORM_BETA * D_FF ** -0.5),
        'b_mlp2': nrm(ks[18], (DEPTH, D_MODEL), 0.02),
        'ln2_g': 1.0 + nrm(ks[19], (DEPTH, D_MODEL), 0.02),
        'ln2_b': nrm(ks[20], (DEPTH, D_MODEL), 0.02),
    }


def reference(x, c, ctx, c_ctx, w_ada, b_ada, w_in, w_gate_up, b_gate, gla_norm_g, w_pool,
              pool_scale, w_out, ln1_g, ln1_b, w_mlp1, b_mlp1, w_mlp2, b_mlp2, ln2_g, ln2_b):
    b, t, _ = x.shape
    ctx_len = ctx.shape[1]
    rows = t // GRID_W
    lat_start = jnp.repeat(jnp.arange(rows, dtype=jnp.int32) * GRID_W, GRID_W)
    ctx_start = jnp.zeros((ctx_len,), jnp.int32)
    s_zero = jnp.zeros((b, GLA_HEADS, GLA_DK, GLA_DV), jnp.float32)
    silu_c = jax.nn.silu(c)
    silu_cc = jax.nn.silu(c_ctx)
    for l in range(DEPTH):
        last = l == DEPTH - 1
        mod = (silu_c @ w_ada[l] + b_ada[l])[:, None, :]
        mod_c = silu_cc @ w_ada[l] + b_ada[l]
        sh1, sc1, g1, sh2, sc2, g2 = jnp.split(mod, 6, axis=-1)
        csh1, csc1, cg1, csh2, csc2, cg2 = jnp.split(mod_c, 6, axis=-1)

        hc = _modulate(ctx, csh1, csc1)
        if last:
            kc, vc, lrc_f, lrc_b = jnp.split(hc @ w_in[l][:, :STATE_COLS], STATE_SPLITS, axis=-1)
            kc_h, vc_h = _heads(kc, GLA_HEADS), _heads(vc, GLA_HEADS)
            s_f = _gla_final_state(kc_h, vc_h, _heads(_log_decay(lrc_f, w_gate_up[l, 0], b_gate[l, 0]), GLA_HEADS))
            s_b = _gla_final_state(_rev(kc_h), _rev(vc_h),
                                   _rev(_heads(_log_decay(lrc_b, w_gate_up[l, 1], b_gate[l, 1]), GLA_HEADS)))
        else:
            mix_c, s_f, s_b = _mixer(hc @ w_in[l], s_zero, s_zero, w_gate_up[l], b_gate[l], gla_norm_g[l],
                                     w_pool[l], pool_scale[l], w_out[l], ctx_start, ctx_len)
            ctx = _layer_norm(DEEPNORM_ALPHA * ctx + cg1 * mix_c, ln1_g[l], ln1_b[l])
            yc = _sq_relu_mlp(_modulate(ctx, csh2, csc2), w_mlp1[l], b_mlp1[l], w_mlp2[l], b_mlp2[l])
            ctx = _layer_norm(DEEPNORM_ALPHA * ctx + cg2 * yc, ln2_g[l], ln2_b[l])

        h = _modulate(x, sh1, sc1)
        mix, _, _ = _mixer(h @ w_in[l], s_f, s_b, w_gate_up[l], b_gate[l], gla_norm_g[l],
                           w_pool[l], pool_scale[l], w_out[l], lat_start, GRID_W)
        x = _layer_norm(DEEPNORM_ALPHA * x + g1 * mix, ln1_g[l], ln1_b[l])
        y = _sq_relu_mlp(_modulate(x, sh2, sc2), w_mlp1[l], b_mlp1[l], w_mlp2[l], b_mlp2[l])
        x = _layer_norm(DEEPNORM_ALPHA * x + g2 * y, ln2_g[l], ln2_b[l])
    return x
```

```python
import numpy as np
from contextlib import ExitStack
import concourse.bass as bass
import concourse.mybir as mybir
from concourse.bass_utils import run_bass_kernel_spmd

F32 = mybir.dt.float32
BF16 = mybir.dt.bfloat16
ALU = mybir.AluOpType
AF = mybir.ActivationFunctionType
AX = mybir.AxisListType

D = 2048
L = 2
NCT = 2
NLT = 32
NT = NCT + NLT
TB = 4
DFF = 8192
ALPHA = float((2 * L) ** 0.25)
QSCALE = float(128 ** -0.5)
EPS = 1e-6
N_CORES = 8
WCACHE = True
SPLIT = True
DEBUG = False
L_RUN = L


def C(name, *a, **k):
    return lambda e: getattr(e, name)(*a, **k)


class Res:
    __slots__ = ("name", "last_w", "readers", "const")

    def __init__(self, name, const=False):
        self.name = name
        self.last_w = None
        self.readers = []
        self.const = const


class Op:
    __slots__ = ("eng", "fn", "deps", "dma", "sem", "cnt", "awaited", "name")


class Prog:
    ENGS = ["tensor", "vector", "scalar", "gpsimd", "sync"]

    def __init__(self, nc, n_dma_sems=(40, 40)):
        self.nc = nc
        self.ops = {e: [] for e in self.ENGS}
        self.n_dma_sems = {"sync": n_dma_sems[0], "gpsimd": n_dma_sems[1]}
        self.dmas_since_barrier = []

    def add(self, eng, fn, reads=(), writes=(), dma=False, name=""):
        op = Op()
        op.eng = eng; op.fn = fn; op.dma = dma; op.awaited = False; op.name = name
        op.sem = None; op.cnt = 0
        deps = set()
        for r in reads:
            if r.last_w is not None:
                deps.add(r.last_w)
            if not r.const:
                r.readers.append(op)
        for r in writes:
            deps.update(r.readers)
            if r.last_w is not None:
                deps.add(r.last_w)
            r.readers = []
            r.last_w = op
        deps.discard(op)
        op.deps = deps
        for d in deps:
            d.awaited = True
        self.ops[eng].append(op)
        if dma:
            self.dmas_since_barrier.append(op)
        return op

    def pe(self, fn, reads=(), writes=()):
        return self.add("tensor", fn, reads, writes)

    def dve(self, fn, reads=(), writes=()):
        return self.add("vector", fn, reads, writes)

    def act(self, fn, reads=(), writes=()):
        return self.add("scalar", fn, reads, writes)

    def dma(self, q, out, in_, reads=(), writes=()):
        return self.add(q, C("dma_start", out=out, in_=in_), reads, writes, dma=True)

    def barrier(self):
        lasts = [self.ops[e][-1] for e in ["tensor", "vector", "scalar"] if self.ops[e]]
        lasts += [o for o in self.ops["gpsimd"][-1:] if not o.dma]
        deps = set(lasts) | set(self.dmas_since_barrier)
        self.dmas_since_barrier = []
        for e in self.ENGS:
            op = self.add(e, lambda g: g.nop())
            op.deps |= {d for d in deps if d is not op}
            for d in op.deps:
                d.awaited = True

    def emit(self, stack):
        nc = self.nc
        esem = {}
        for e in ["tensor", "vector", "scalar", "gpsimd"]:
            esem[e] = stack.enter_context(nc.semaphore("s_" + e))
        dsem = {q: [stack.enter_context(nc.semaphore(f"d_{q}_{i}")) for i in range(n)]
                for q, n in self.n_dma_sems.items()}
        for e in self.ENGS:
            c = 0
            rr = 0
            use = {}
            prev_user = {}
            for op in self.ops[e]:
                if op.dma:
                    pool = dsem[e]
                    k = rr % len(pool)
                    rr += 1
                    use[k] = use.get(k, 0) + 1
                    op.sem = pool[k]
                    op.cnt = 16 * use[k]
                    if k in prev_user:
                        op.deps.add(prev_user[k])
                    prev_user[k] = op
                    op.awaited = True
                else:
                    if e == "sync":
                        continue
                    if op.awaited:
                        c += 1
                    op.sem = esem[e]
                    op.cnt = c
        block = stack.enter_context(nc.Block())

        def make(e):
            def body(eng):
                known = {}
                for op in self.ops[e]:
                    need = {}
                    for d in op.deps:
                        if d.sem is None:
                            continue
                        if (not d.dma) and d.eng == e and e == "tensor":
                            continue
                        key = id(d.sem)
                        if key not in need or need[key][1] < d.cnt:
                            need[key] = (d.sem, d.cnt)
                    for key, (s, cnt) in need.items():
                        if known.get(key, 0) >= cnt:
                            continue
                        eng.wait_ge(s, cnt)
                        known[key] = cnt
                    ins = op.fn(eng)
                    if op.dma:
                        ins.then_inc(op.sem, 16)
                    elif op.awaited and op.sem is not None:
                        ins.then_inc(op.sem, 1)
            return body

        block.tensor(make("tensor"))
        block.vector(make("vector"))
        block.scalar(make("scalar"))
        block.gpsimd(make("gpsimd"))
        block.sync(make("sync"))


def build_nc():
    nc = bass.Bass("TRN2", target_bir_lowering=False)
    P = Prog(nc)

    def din(name, shape, dt=F32):
        return nc.dram_tensor(name, list(shape), dt, kind="ExternalInput").ap()

    def dscr(name, shape=None, dt=F32, dbg=False):
        if dbg and DEBUG:
            return nc.dram_tensor(name, list(shape), dt, kind="ExternalOutput").ap()
        return nc.dram_tensor(name, list(shape), dt).ap()

    xin = din("xin", [NT * 128, D])
    cvec = din("cvec", [32, 128])
    w_ada = din("w_ada", [L, D, 6 * D])
    b_ada = din("b_ada", [L, 6 * D])
    w_fm = din("w_fm", [L, D, 1088])
    w_tm = din("w_tm", [L, D, 3584])
    w_gate = din("w_gate", [L, 2, 16, 512])
    b_gate = din("b_gate", [L, 1, 1024])
    normg = din("normg", [L, 256])
    w_pool = din("w_pool", [L, 4, 256, 256])
    pool_scale = din("pool_scale", [L, 1024])
    w_out = din("w_out", [L, D, D])
    ln1_g = din("ln1_g", [L, D]); ln1_b = din("ln1_b", [L, D])
    w1 = din("w1", [L, D, DFF]); b1 = din("b1", [L, DFF])
    w2 = din("w2", [L, DFF, D]); b2 = din("b2", [L, D])
    ln2_g = din("ln2_g", [L, D]); ln2_b = din("ln2_b", [L, D])
    c_ident = din("c_ident", [128, 128])
    c_tri = din("c_tri", [128, 4, 128])
    c_mask = din("c_mask", [128, 2, 128])
    c_poolL = din("c_poolL", [128, 4, 128])
    c_poolC = din("c_poolC", [128, 16, 128])
    c_oneh = din("c_oneh", [4, 4, 128])
    NOWN = NLT // 2 if SPLIT else NLT
    y = nc.dram_tensor("y", [NOWN * 128, D], F32, kind="ExternalOutput").ap()

    X1s = dscr("X1s", [NT * 128, D], dbg=True)
    X2s = dscr("X2s", [NT * 128, D], dbg=True)
    modrow = dscr("modrow", [L, 2, 6 * D], dbg=True)
    st_qt = dscr("st_qt", dbg=True, shape=[NT, 128, 512], dt=BF16)
    st_kt = dscr("st_kt", dbg=True, shape=[NT, 128, 512], dt=BF16)
    st_kh = dscr("st_kh", dbg=True, shape=[NT, 128, 512], dt=BF16)
    st_et = dscr("st_et", dbg=True, shape=[NT, 128, 4])
    st_v = dscr("st_v", dbg=True, shape=[NT, 128, 1024], dt=BF16)
    st_sg = dscr("st_sg", dbg=True, shape=[NT, 128, 1024])
    st_oA = dscr("st_oA", dbg=True, shape=[NT, 128, 1024])
    st_mp = dscr("st_mp", dbg=True, shape=[NT, 128, 1024], dt=BF16)
    wscr = dscr("wscr", shape=[96, 128, 16 * 512], dt=BF16) if WCACHE else None
    dbg_mix = dscr("dbg_mix", dbg=True, shape=[NT, 128, 1024])
    dbg_o = dscr("dbg_o", dbg=True, shape=[NT, 128, 1024])
    dbg_r = dscr("dbg_r", dbg=True, shape=[NT * 128, D])

    with ExitStack() as st:
        def sb(name, shape, dt=F32):
            return st.enter_context(nc.sbuf_tensor(name, list(shape), dt))

        ident = sb("ident", [128, 128]); R_ident = Res("ident", const=True)
        ones1 = sb("ones1", [1, 128]); R_ones1 = Res("ones1", const=True)
        oneh = sb("oneh", [4, 4, 128]); R_oneh = Res("oneh", const=True)
        cv = sb("cv", [32, 128]); R_cv = Res("cv")
        silT = sb("silT", [128, 16, 2], BF16); R_silT = Res("silT")
        modT = sb("modT", [128, 2, 96]); R_modT = Res("modT")
        opsc = sb("opsc", [128, 2, 2, 16]); R_opsc = Res("opsc")
        m96 = sb("m96", [96, 128]); R_m96 = Res("m96")
        b1T = sb("b1T", [128, 64]); R_b1T = Res("b1T")
        b1l = sb("b1l", [64, 128]); R_b1l = Res("b1l")
        b2r = sb("b2r", [4, 512]); R_b2r = Res("b2r")
        poolsT = sb("poolsT", [128, 8]); R_poolsT = Res("poolsT")
        psl = sb("psl", [8, 128]); R_psl = Res("psl")
        badd = [sb(f"badd{i}", [2, 512]) for i in range(2)]; R_badd = [Res(f"badd{i}") for i in range(2)]
        mrow = [sb(f"mrow{i}", [2, 512]) for i in range(2)]; R_mrow = [Res(f"mrow{i}") for i in range(2)]
        actT = sb("actT", [128, 16, 512], BF16); R_actT = Res("actT")
        wbufs = [sb(f"wbuf{i}", [128, 16, 512], BF16) for i in range(2)]
        R_wbufs = [Res(f"wbuf{i}") for i in range(2)]
        stt = sb("stt", [128, 8]); R_stt = Res("stt")
        ARENA_BYTES = 136 * 1024
        arena = sb("arena", [128, ARENA_BYTES // 2], dt=BF16)
        psb = [st.enter_context(nc.psum_tensor(f"psb{i}", [128, 512], F32)) for i in range(8)]
        R_ps = [Res(f"ps{i}") for i in range(8)]

        state = {"ps": 0, "wb": 0, "aoff": 0, "ps_pool": list(range(8))}

        def psum():
            pool = state["ps_pool"]
            i = pool[state["ps"] % len(pool)]
            state["ps"] += 1
            return psb[i], R_ps[i]

        def wbuf():
            i = state["wb"] % 2
            state["wb"] += 1
            return wbufs[i], R_wbufs[i]

        def aalloc(name, shape, dt=F32):
            esz = 4 if dt == F32 else 2
            n = 1
            for s in shape[1:]:
                n *= s
            nbytes = n * esz
            off = state["aoff"]
            off = (off + 63) // 64 * 64
            assert off + nbytes <= ARENA_BYTES, (name, off, nbytes)
            state["aoff"] = off + nbytes
            v = arena[0:shape[0], off // 2:(off + nbytes) // 2]
            if dt == F32:
                v = v.bitcast(F32)
            if len(shape) == 3:
                v = v.rearrange("p (a b) -> p a b", b=shape[2])
            return v, Res(name)

        def areset(mark=0):
            state["aoff"] = mark

        wcache = {}

        def wload(key, src_ap, ncol=512):
            wb, rw = wbuf()
            if key is None or not WCACHE:
                P.dma("gpsimd", wb[:, :, 0:ncol], src_ap, writes=[rw])
            elif key in wcache:
                idx, rres = wcache[key]
                P.dma("gpsimd", wb[:, :, 0:ncol], wscr[idx].rearrange("p (a b) -> p a b", b=512)[:, :, 0:ncol], reads=[rres], writes=[rw])
            else:
                idx = len(wcache)
                rres = Res(f"wscr{idx}")
                wcache[key] = (idx, rres)
                P.dma("gpsimd", wb[:, :, 0:ncol], src_ap, writes=[rw])
                P.dma("sync", wscr[idx].rearrange("p (a b) -> p a b", b=512)[:, :, 0:ncol], wb[:, :, 0:ncol], reads=[rw], writes=[rres])
            return wb, rw

        def wview(w2d):
            return w2d.rearrange("(kc p) n -> p kc n", p=128)

        P.dma("sync", ident[:], c_ident[:, :], writes=[R_ident])
        P.dma("sync", oneh[:], c_oneh[:, :, :], writes=[R_oneh])
        P.dve(C("memset", ones1[:], 1.0), writes=[R_ones1])
        P.dma("sync", cv[:], cvec[:, :], writes=[R_cv])
        P.act(C("activation", out=cv[:], in_=cv[:], func=AF.Silu), reads=[R_cv], writes=[R_cv])
        ps, rp = psum()
        P.pe(C("transpose", out=ps[:, 0:32], in_=cv[0:32, :], identity=ident[0:32, 0:32]),
             reads=[R_cv, R_ident], writes=[rp])
        for v in range(2):
            P.dve(C("tensor_copy", out=silT[:, :, v], in_=ps[:, v * 16:(v + 1) * 16]),
                  reads=[rp], writes=[R_silT])

        def ln_stats(x_ap, rx, junk=None, rj=None):
            if junk is None:
                junk, rj = aalloc_junk()
            P.dve(C("memset", stt[:, 1:2], 0.0), writes=[R_stt])
            P.dve(C("reduce_sum", out=stt[:, 0:1], in_=x_ap, axis=AX.X), reads=[rx], writes=[R_stt])
            P.act(C("activation", out=junk, in_=x_ap, func=AF.Square, accum_out=stt[:, 1:2]),
                  reads=[rx], writes=[rj, R_stt])
            P.dve(C("tensor_scalar", out=stt[:, 0:1], in0=stt[:, 0:1], scalar1=1.0 / D, scalar2=None,
                                            op0=ALU.mult), reads=[R_stt], writes=[R_stt])
            P.dve(C("tensor_tensor", out=stt[:, 2:3], in0=stt[:, 0:1], in1=stt[:, 0:1], op=ALU.mult),
                  reads=[R_stt], writes=[R_stt])
            P.dve(C("scalar_tensor_tensor", out=stt[:, 3:4], in0=stt[:, 1:2], scalar=1.0 / D,
                                                   in1=stt[:, 2:3], op0=ALU.mult, op1=ALU.subtract),
                  reads=[R_stt], writes=[R_stt])
            P.dve(C("tensor_scalar", out=stt[:, 3:4], in0=stt[:, 3:4], scalar1=EPS, scalar2=None,
                                            op0=ALU.add), reads=[R_stt], writes=[R_stt])
            P.act(C("activation", out=stt[:, 3:4], in_=stt[:, 3:4], func=AF.Ln), reads=[R_stt], writes=[R_stt])
            P.act(C("activation", out=stt[:, 3:4], in_=stt[:, 3:4], func=AF.Exp, scale=-0.5),
                  reads=[R_stt], writes=[R_stt])

        junk_holder = {}

        def aalloc_junk():
            return junk_holder["junk"], junk_holder["rj"]

        def normalize(out_ap, rout, x_ap, rx):
            P.dve(C("tensor_scalar", out=out_ap, in0=x_ap, scalar1=stt[:, 0:1], scalar2=stt[:, 3:4],
                                            op0=ALU.subtract, op1=ALU.mult),
                  reads=[rx, R_stt], writes=[rout])

        def to_actT(xn_ap, rxn, ti, v, which):
            for q4 in range(4):
                ps, rp = psum()
                for j in range(4):
                    dc = q4 * 4 + j
                    P.pe(C("transpose", out=ps[:, j * 128:(j + 1) * 128],
                                                                 in_=xn_ap[:, dc * 128:(dc + 1) * 128],
                                                                 identity=ident[:]),
                         reads=[rxn, R_ident], writes=[rp])
                for j in range(4):
                    dc = q4 * 4 + j
                    P.act(C("activation",
                        out=actT[:, dc, ti * 128:(ti + 1) * 128], in_=ps[:, j * 128:(j + 1) * 128],
                        func=AF.Identity, bias=modT[:, v, (3 * which) * 16 + dc:(3 * which) * 16 + dc + 1],
                        scale=opsc[:, v, which, dc:dc + 1]),
                        reads=[rp, R_modT, R_opsc], writes=[R_actT])

        def vec_cols(dst, rdst, tmp, rtmp, src_2d, nrow):
            P.dma("sync", tmp[0:nrow, :], src_2d, writes=[rtmp])
            ps, rp = psum()
            P.pe(C("transpose", out=ps[:, 0:nrow], in_=tmp[0:nrow, :], identity=ident[0:nrow, 0:nrow]),
                 reads=[rtmp, R_ident], writes=[rp])
            P.dve(C("tensor_copy", out=dst, in_=ps[:, 0:nrow]), reads=[rp], writes=[rdst])

        def scan_tile(qt, kt, kh, et, v_ap, S, Sb, PT, mask, rin, rS, rSb, rPT, rmask, ofix=None):
            for h in range(4):
                ps, rp = psum()
                P.pe(C("matmul", ps[:, 0:128], lhsT=kt[:, h, :], rhs=qt[:, h, :],
                                                    start=True, stop=True), reads=rin, writes=[rp])
                P.dve(C("tensor_tensor", out=PT[:, h, :], in0=ps[:, 0:128], in1=mask,
                                                            op=ALU.mult), reads=[rp, rmask], writes=[rPT])
            obanks = []
            for hp in range(2):
                ps, rp = psum() if ofix is None else (psb[ofix[hp]], R_ps[ofix[hp]])
                for hh in range(2):
                    h = hp * 2 + hh
                    P.pe(C("matmul", ps[:, hh * 256:(hh + 1) * 256], lhsT=PT[:, h, :],
                                                               rhs=v_ap[:, h * 256:(h + 1) * 256],
                                                               start=True, stop=False),
                         reads=rin + [rPT], writes=[rp])
                    P.pe(C("matmul", ps[:, hh * 256:(hh + 1) * 256], lhsT=qt[:, h, :],
                                                               rhs=Sb[:, h, :], start=False, stop=True),
                         reads=rin + [rSb], writes=[rp])
                obanks.append((ps, rp))
            state_update(kh, et, v_ap, S, Sb, rin, rS, rSb)
            return obanks

        def state_update(kh, et, v_ap, S, Sb, rin, rS, rSb):
            for hp in range(2):
                ps, rp = psum()
                for hh in range(2):
                    h = hp * 2 + hh
                    P.pe(C("matmul", ps[:, hh * 256:(hh + 1) * 256],
                                                               lhsT=kh[:, h * 128:(h + 1) * 128],
                                                               rhs=v_ap[:, h * 256:(h + 1) * 256],
                                                               start=True, stop=True), reads=rin, writes=[rp])
                for hh in range(2):
                    h = hp * 2 + hh
                    P.dve(C("scalar_tensor_tensor",
                        out=S[:, h, :], in0=S[:, h, :], scalar=et[:, h:h + 1], in1=ps[:, hh * 256:(hh + 1) * 256],
                        op0=ALU.mult, op1=ALU.add), reads=[rp, rS] + rin, writes=[rS])
                    P.act(C("copy", out=Sb[:, h, :], in_=S[:, h, :]), reads=[rS], writes=[rSb])

        blocks_ctx = [list(range(0, NCT))]
        blocks_lat = [list(range(NCT + b * TB, NCT + (b + 1) * TB)) for b in range(NLT // TB)]
        n_own = (NLT // TB) // 2 if SPLIT else NLT // TB
        blocks_own = blocks_lat[:n_own]
        blocks_oth = blocks_lat[n_own:]

        for l in range(L_RUN):
            last = (l == L - 1)
            Xin = xin if l == 0 else X2s
            areset()
            for j in range(24):
                wb, rw = wload(None, wview(w_ada[l])[:, :, j * 512:(j + 1) * 512])
                bi = j % 2
                P.dma("sync", badd[bi][:], b_ada[l:l + 1, j * 512:(j + 1) * 512].partition_broadcast(2),
                      writes=[R_badd[bi]])
                ps, rp = psum()
                for kc in range(16):
                    P.pe(C("matmul", ps[0:2, :], lhsT=silT[:, kc, :], rhs=wb[:, kc, :],
                           start=(kc == 0), stop=(kc == 15)), reads=[R_silT, rw], writes=[rp])
                P.dve(C("tensor_tensor", out=mrow[bi][:], in0=ps[0:2, :], in1=badd[bi][:], op=ALU.add),
                      reads=[rp, R_badd[bi]], writes=[R_mrow[bi]])
                P.dma("sync", modrow[l, :, j * 512:(j + 1) * 512], mrow[bi][:], reads=[R_mrow[bi]])
            P.barrier()
            for v in range(2):
                vec_cols(modT[:, v, :], R_modT, m96, R_m96,
                         modrow[l, v, :].rearrange("(r p) -> r p", p=128), 96)
            for v in range(2):
                for which in range(2):
                    P.dve(C("tensor_scalar", out=opsc[:, v, which, :],
                            in0=modT[:, v, (3 * which + 1) * 16:(3 * which + 2) * 16],
                            scalar1=1.0, scalar2=None, op0=ALU.add), reads=[R_modT], writes=[R_opsc])
            vec_cols(b1T[:], R_b1T, b1l, R_b1l, b1[l, :].rearrange("(r p) -> r p", p=128), 64)
            vec_cols(poolsT[:], R_poolsT, psl, R_psl, pool_scale[l, :].rearrange("(r p) -> r p", p=128), 8)
            P.dma("sync", b2r[:], b2[l, :].rearrange("(r n) -> r n", n=512), writes=[R_b2r])
            P.barrier()

            areset()
            tri, R_tri = aalloc("tri", [128, 4, 128]); R_tri.const = True
            msk, R_msk = aalloc("msk", [128, 2, 128]); R_msk.const = True
            poolL, R_poolL = aalloc("poolL", [128, 4, 128]); R_poolL.const = True
            poolC, R_poolC = aalloc("poolC", [128, 16, 128]); R_poolC.const = True
            wpool, R_wpool = aalloc("wpool", [128, 8, 256], BF16); R_wpool.const = True
            wgA, R_wgate = aalloc("wgA", [16, 512]); R_wgate.const = True
            wgB, _ = aalloc("wgB", [16, 512])
            wgd = [wgA, wgB]
            bgate, R_bgate = aalloc("bgate", [1, 1024]); R_bgate.const = True
            normgb, R_normgb = aalloc("normgb", [128, 256]); R_normgb.const = True
            SA, R_SA = aalloc("SA", [128, 4, 256]); SAb, R_SAb = aalloc("SAb", [128, 4, 256], BF16)
            SB, R_SB = aalloc("SB", [128, 4, 256]); SBb, R_SBb = aalloc("SBb", [128, 4, 256], BF16)
            mark12 = state["aoff"]
            P.dma("sync", tri, c_tri[:, :, :], writes=[R_tri])
            P.dma("sync", msk, c_mask[:, :, :], writes=[R_msk])
            P.dma("sync", poolL, c_poolL[:, :, :], writes=[R_poolL])
            P.dma("sync", poolC, c_poolC[:, :, :], writes=[R_poolC])
            P.dma("gpsimd", wpool.rearrange("p (g c) d -> p g c d", c=2),
                  w_pool[l].rearrange("g (c p) d -> p g c d", p=128), writes=[R_wpool])
            P.dma("sync", wgA, w_gate[l, 0, :, :], writes=[R_wgate])
            P.dma("sync", wgB, w_gate[l, 1, :, :], writes=[R_wgate])
            P.dma("sync", bgate, b_gate[l, :, :], writes=[R_bgate])
            P.dma("sync", normgb, normg[l:l + 1, :].partition_broadcast(128), writes=[R_normgb])
            for S_, r_ in ((SA, R_SA), (SB, R_SB), (SAb, R_SAb), (SBb, R_SBb)):
                P.dve(C("memset", S_, 0.0), writes=[r_])

            areset(mark12)
            xt = [aalloc(f"xt{i}", [128, D]) for i in range(2)]
            junk_holder["junk"], junk_holder["rj"] = aalloc("junk", [128, D])
            qT, R_qT = aalloc("qT", [128, 4, 512], BF16)
            kT, R_kT = aalloc("kT", [128, 4, 512], BF16)
            lrA, R_lrT = aalloc("lrA", [16, 512])
            lrB, _ = aalloc("lrB", [16, 512])
            lrd = [lrA, lrB]
            ktm, R_ktm = aalloc("ktm", [128, TB, 512], BF16)
            vb, R_vb = aalloc("vb", [128, TB, 1024], BF16)
            ub, R_ub = aalloc("ub", [128, TB, 1024])
            sgr = [aalloc(f"sgr{i}", [128, 512]) for i in range(2)]
            tmpg, R_tmpg = aalloc("tmpg", [128, 512])
            spd = [aalloc(f"spd{i}", [128, 512]) for i in range(2)]
            ecum, R_ecum = aalloc("ecum", [128, 4, 128])
            einv, R_einv = aalloc("einv", [128, 4, 128])
            erev, R_erev = aalloc("erev", [128, 512])
            qtd = [aalloc(f"qtd{i}", [128, 4, 128], BF16) for i in range(2)]
            ktd = [aalloc(f"ktd{i}", [128, 4, 128], BF16) for i in range(2)]
            khd = [aalloc(f"khd{i}", [128, 512], BF16) for i in range(2)]
            etd = [aalloc(f"etd{i}", [128, 4]) for i in range(2)]
            PT, R_PT = aalloc("PT", [128, 4, 128], BF16)
            oA, R_oA = aalloc("oA", [128, 1024])
            residT, R_residT = aalloc("residT", [128, 8, 128], BF16)
            mp, R_mp = aalloc("mp", [128, 8, 128], BF16)
            cnt = {"x": 0, "sg": 0}

            def gates(ti, d, full):
                gates_pre(ti, d)
                gates_post(ti, d, full)

            def gates_pre(ti, d):
                tsl = slice(ti * 128, (ti + 1) * 128)
                ps, rp = psum()
                P.pe(C("matmul", ps[:, :], lhsT=lrd[d][0:16, tsl], rhs=wgd[d][0:16, :],
                       start=True, stop=False), reads=[R_lrT, R_wgate], writes=[rp])
                P.pe(C("matmul", ps[:, :], lhsT=ones1[0:1, :], rhs=bgate[0:1, d * 512:(d + 1) * 512],
                       start=False, stop=True), reads=[R_ones1, R_bgate], writes=[rp])
                sp_, rsp = spd[d]
                P.act(C("activation", out=tmpg, in_=ps[:, :], func=AF.Exp, scale=-1.0), reads=[rp], writes=[R_tmpg])
                P.act(C("activation", out=sp_, in_=tmpg, func=AF.Ln, bias=1.0, scale=1.0),
                      reads=[R_tmpg], writes=[rsp])

            def gates_post(ti, d, full):
                tsl = slice(ti * 128, (ti + 1) * 128)
                sp_, rsp = spd[d]
                ps, rp = psum()
                for h in range(4):
                    P.pe(C("matmul", ps[:, h * 128:(h + 1) * 128], lhsT=sp_[:, h * 128:(h + 1) * 128],
                           rhs=tri[:, 2 * d, :], start=True, stop=True), reads=[rsp, R_tri], writes=[rp])
                P.act(C("activation", out=ecum.rearrange("p a b -> p (a b)"), in_=ps[:, :], func=AF.Exp),
                      reads=[rp], writes=[R_ecum])
                qt_, rqt = qtd[d]; kt_, rkt = ktd[d]; kh_, rkh = khd[d]; et_, ret = etd[d]
                if full:
                    P.act(C("activation", out=einv.rearrange("p a b -> p (a b)"), in_=ps[:, :], func=AF.Exp,
                            scale=-1.0), reads=[rp], writes=[R_einv])
                    P.dve(C("tensor_tensor", out=qt_, in0=qT[:, :, tsl], in1=ecum, op=ALU.mult),
                          reads=[R_qT, R_ecum], writes=[rqt])
                    P.dve(C("tensor_tensor", out=kt_, in0=kT[:, :, tsl], in1=einv, op=ALU.mult),
                          reads=[R_kT, R_einv], writes=[rkt])
                lastcol = 127 if d == 0 else 0
                P.dve(C("tensor_copy", out=et_, in_=ecum[:, :, lastcol]), reads=[R_ecum], writes=[ret])
                ps, rp = psum()
                P.pe(C("matmul", ps[:, :], lhsT=tri[:, 2 * d + 1, :], rhs=sp_, start=True, stop=True),
                     reads=[rsp, R_tri], writes=[rp])
                P.act(C("activation", out=erev, in_=ps[:, :], func=AF.Exp), reads=[rp], writes=[R_erev])
                P.dve(C("tensor_tensor", out=kh_, in0=ktm[:, ti, :], in1=erev, op=ALU.mult),
                      reads=[R_ktm, R_erev], writes=[rkh])

            def pool_a(ti, t, is_ctx):
                for half in range(2):
                    ps, rp = psum()
                    for k in range(4):
                        gi = half * 2 + k // 2
                        ch = k % 2
                        srcs = [(ti, poolL[:, gi, :], R_poolL)] if not is_ctx else \
                            [(tj, poolC[:, gi * 4 + tj * 2 + ti, :], R_poolC) for tj in range(2)]
                        for si, (tj, pm, rpm) in enumerate(srcs):
                            P.pe(C("matmul", ps[:, k * 128:(k + 1) * 128],
                                   lhsT=ub[:, tj, gi * 256 + ch * 128:gi * 256 + (ch + 1) * 128], rhs=pm,
                                   start=(si == 0), stop=(si == len(srcs) - 1)), reads=[R_ub, rpm], writes=[rp])
                    P.dve(C("tensor_copy", out=residT[:, half * 4:(half + 1) * 4, :].rearrange("p a b -> p (a b)"),
                            in_=ps[:, :]), reads=[rp], writes=[R_residT])

            def pool_b(ti, t):
                for half in range(2):
                    ps, rp = psum()
                    for k in range(4):
                        gi = half * 2 + k // 2
                        dh = k % 2
                        for ch in range(2):
                            P.pe(C("matmul", ps[:, k * 128:(k + 1) * 128],
                                   lhsT=wpool[:, gi * 2 + ch, dh * 128:(dh + 1) * 128], rhs=residT[:, gi * 2 + ch, :],
                                   start=(ch == 0), stop=(ch == 1)), reads=[R_wpool, R_residT], writes=[rp])
                    for k in range(4):
                        gi = half * 2 + k // 2
                        dh = k % 2
                        P.act(C("activation", out=mp[:, gi * 2 + dh, :], in_=ps[:, k * 128:(k + 1) * 128],
                                func=AF.Identity, scale=poolsT[:, gi * 2 + dh:gi * 2 + dh + 1]),
                              reads=[rp, R_poolsT], writes=[R_mp])
                P.dma("sync", st_mp[t, :, :], mp.rearrange("p a b -> p (a b)"), reads=[R_mp])

            def p1_block(blk, mode):
                full = (mode == "full")
                is_ctx = blk[0] < NCT
                v_ = 1 if is_ctx else 0
                nb = len(blk)
                N = nb * 128
                for ti, t in enumerate(blk):
                    xtile, rx = xt[cnt["x"] % 2]; cnt["x"] += 1
                    P.dma("sync", xtile, Xin[t * 128:(t + 1) * 128, :], writes=[rx])
                    ln_stats(xtile, rx)
                    normalize(xtile, rx, xtile, rx)
                    to_actT(xtile, rx, ti, v_, 0)
                yield
                for part in (range(2) if full else []):
                    wb, rw = wload(("fm", l, part), wview(w_fm[l])[:, :, part * 512:(part + 1) * 512])
                    for h in range(4):
                        ps, rp = psum()
                        for kc in range(16):
                            P.pe(C("matmul", ps[:, 0:N], lhsT=wb[:, kc, h * 128:(h + 1) * 128], rhs=actT[:, kc, 0:N],
                                   start=(kc == 0), stop=(kc == 15)), reads=[rw, R_actT], writes=[rp])
                        if part == 0:
                            P.act(C("activation", out=qT[:, h, 0:N], in_=ps[:, 0:N], func=AF.Identity, scale=QSCALE),
                                  reads=[rp], writes=[R_qT])
                        else:
                            P.dve(C("tensor_copy", out=kT[:, h, 0:N], in_=ps[:, 0:N]), reads=[rp], writes=[R_kT])
                wb, rw = wload(("fm", l, 2), wview(w_fm[l])[:, :, 1024:1088], ncol=64)
                for d in range(2):
                    ps, rp = psum()
                    for kc in range(16):
                        P.pe(C("matmul", ps[0:16, 0:N], lhsT=wb[:, kc, 32 * d:32 * d + 16], rhs=actT[:, kc, 0:N],
                               start=(kc == 0), stop=(kc == 15)), reads=[rw, R_actT], writes=[rp])
                    P.dve(C("tensor_copy", out=lrd[d][:, 0:N], in_=ps[0:16, 0:N]), reads=[rp], writes=[R_lrT])
                for c in range(7 if full else 3):
                    wb, rw = wload(("tm", l, c), wview(w_tm[l])[:, :, c * 512:(c + 1) * 512])
                    for ti, t in enumerate(blk):
                        ps, rp = psum()
                        for kc in range(16):
                            P.pe(C("matmul", ps[:, :], lhsT=actT[:, kc, ti * 128:(ti + 1) * 128], rhs=wb[:, kc, :],
                                   start=(kc == 0), stop=(kc == 15)), reads=[rw, R_actT], writes=[rp])
                        if c == 0:
                            P.dve(C("tensor_copy", out=ktm[:, ti, :], in_=ps[:, :]), reads=[rp], writes=[R_ktm])
                        elif c in (1, 2):
                            P.act(C("copy", out=vb[:, ti, (c - 1) * 512:c * 512], in_=ps[:, :]),
                                  reads=[rp], writes=[R_vb])
                        elif c in (3, 4):
                            sg_, rsg = sgr[cnt["sg"] % 2]; cnt["sg"] += 1
                            P.act(C("activation", out=sg_, in_=ps[:, :], func=AF.Silu), reads=[rp], writes=[rsg])
                            P.dma("sync", st_sg[t, :, (c - 3) * 512:(c - 2) * 512], sg_, reads=[rsg])
                        else:
                            P.dve(C("tensor_copy", out=ub[:, ti, (c - 5) * 512:(c - 4) * 512], in_=ps[:, :]),
                                  reads=[rp], writes=[R_ub])
                yield
                if not full:
                    if mode == "stateAB":
                        for ti in range(nb):
                            gates(ti, 0, False)
                            state_update(khd[0][0], etd[0][0], vb[:, ti, :], SA, SAb,
                                         [khd[0][1], etd[0][1], R_vb], R_SA, R_SAb)
                    for ti in range(nb - 1, -1, -1):
                        gates(ti, 1, False)
                        state_update(khd[1][0], etd[1][0], vb[:, ti, :], SB, SBb,
                                     [khd[1][1], etd[1][1], R_vb], R_SB, R_SBb)
                    return
                for ti, t in enumerate(blk):
                    for d in range(2):
                        gates_pre(ti, d)
                    pool_a(ti, t, is_ctx)
                    for d in range(2):
                        gates_post(ti, d, True)
                    pool_b(ti, t)
                    qt_, rqt = qtd[0]; kt_, rkt = ktd[0]; kh_, rkh = khd[0]; et_, ret = etd[0]
                    ob = scan_tile(qt_, kt_, kh_, et_, vb[:, ti, :], SA, SAb, PT, msk[:, 0, :],
                                   [rqt, rkt, rkh, ret, R_vb], R_SA, R_SAb, R_PT, R_msk)
                    for hp, (ps, rp) in enumerate(ob):
                        P.act(C("copy", out=oA[:, hp * 512:(hp + 1) * 512], in_=ps[:, :]), reads=[rp], writes=[R_oA])
                    P.dma("sync", st_oA[t, :, :], oA, reads=[R_oA])
                    qt_, rqt = qtd[1]; kt_, rkt = ktd[1]; kh_, rkh = khd[1]; et_, ret = etd[1]
                    P.dma("sync", st_qt[t, :, :], qt_.rearrange("p a b -> p (a b)"), reads=[rqt])
                    P.dma("sync", st_kt[t, :, :], kt_.rearrange("p a b -> p (a b)"), reads=[rkt])
                    P.dma("sync", st_kh[t, :, :], kh_, reads=[rkh])
                    P.dma("sync", st_et[t, :, :], et_, reads=[ret])
                    P.dma("sync", st_v[t, :, :], vb[:, ti, :], reads=[R_vb])

            if not last:
                p1_seq = [(blk, "full") for blk in blocks_ctx + blocks_lat]
                p2_blocks = blocks_ctx + blocks_lat[::-1]
                p3_blocks = blocks_lat + blocks_ctx
            else:
                p1_seq = [(blocks_ctx[0], "stateAB")] + [(blk, "full") for blk in blocks_own] + \
                         [(blk, "stateB") for blk in blocks_oth[::-1]]
                p2_blocks = blocks_own[::-1]
                p3_blocks = blocks_own
            gens = [p1_block(blk, mode) for blk, mode in p1_seq]
            next(gens[0])
            for gi_, g_ in enumerate(gens):
                next(g_)
                if gi_ + 1 < len(gens):
                    next(gens[gi_ + 1])
                for _ in g_:
                    pass
            P.barrier()

            areset(mark12)
            gbc, R_gbc = aalloc("gbc", [128, D]); lng, R_lng = aalloc("lng", [128, D]); lnb, R_lnb = aalloc("lnb", [128, D])
            junk_holder["junk"], junk_holder["rj"] = aalloc("junk2", [128, D], BF16)
            xblk, R_xblk = aalloc("xblk", [128, TB, D])
            tmp5, R_tmp5 = aalloc("tmp5", [128, 512])
            ldA = []
            for i in range(3):
                ldA.append(dict(
                    qt=aalloc(f"l_qt{i}", [128, 4, 128], BF16), kt=aalloc(f"l_kt{i}", [128, 4, 128], BF16),
                    kh=aalloc(f"l_kh{i}", [128, 512], BF16), et=aalloc(f"l_et{i}", [128, 4]),
                    v=aalloc(f"l_v{i}", [128, 1024], BF16)))
            ldB = []
            for i in range(2):
                ldB.append(dict(oA=aalloc(f"l_oA{i}", [128, 1024]), sg=aalloc(f"l_sg{i}", [128, 1024])))
            PT, R_PT = aalloc("PT2", [128, 4, 128], BF16)
            P.dma("sync", lng, ln1_g[l:l + 1, :].partition_broadcast(128), writes=[R_lng])
            P.dma("sync", lnb, ln1_b[l:l + 1, :].partition_broadcast(128), writes=[R_lnb])
            cntA = {"a": 0, "b": 0}

            def p2_loadA(t):
                a_ = ldA[cntA["a"] % 3]; cntA["a"] += 1
                P.dma("sync", a_["qt"][0].rearrange("p a b -> p (a b)"), st_qt[t, :, :], writes=[a_["qt"][1]])
                P.dma("sync", a_["kt"][0].rearrange("p a b -> p (a b)"), st_kt[t, :, :], writes=[a_["kt"][1]])
                P.dma("sync", a_["kh"][0], st_kh[t, :, :], writes=[a_["kh"][1]])
                P.dma("sync", a_["et"][0], st_et[t, :, :], writes=[a_["et"][1]])
                P.dma("sync", a_["v"][0], st_v[t, :, :], writes=[a_["v"][1]])
                return a_

            def p2_front(ti, t, a_):
                b_ = ldB[cntA["b"] % 2]; cntA["b"] += 1
                P.dma("sync", b_["oA"][0], st_oA[t, :, :], writes=[b_["oA"][1]])
                P.dma("sync", b_["sg"][0], st_sg[t, :, :], writes=[b_["sg"][1]])
                ob = scan_tile(a_["qt"][0], a_["kt"][0], a_["kh"][0], a_["et"][0], a_["v"][0], SB, SBb, PT,
                               msk[:, 1, :], [a_["qt"][1], a_["kt"][1], a_["kh"][1], a_["et"][1], a_["v"][1]],
                               R_SB, R_SBb, R_PT, R_msk, ofix=(4, 5) if (cntA["b"] % 2) else (6, 7))
                return b_, ob

            def p2_back(ti, t, b_, ob):
                P.dma("sync", actT[:, 8:16, ti * 128:(ti + 1) * 128],
                      st_mp[t, :, :].rearrange("p (a b) -> p a b", b=128), writes=[R_actT])
                o_, ro = b_["oA"]
                sg_, rsg = b_["sg"]
                for hp, (ps, rp) in enumerate(ob):
                    P.dve(C("tensor_tensor", out=o_[:, hp * 512:(hp + 1) * 512], in0=o_[:, hp * 512:(hp + 1) * 512],
                            in1=ps[:, :], op=ALU.add), reads=[rp, ro], writes=[ro])
                jk, rj = junk_holder["junk"], junk_holder["rj"]
                P.dve(C("memset", stt[:, 4:8], 0.0), writes=[R_stt])
                for h in range(4):
                    P.act(C("activation", out=jk[:, h * 256:(h + 1) * 256], in_=o_[:, h * 256:(h + 1) * 256],
                            func=AF.Square, accum_out=stt[:, 4 + h:5 + h]), reads=[ro], writes=[rj, R_stt])
                P.dve(C("tensor_scalar", out=stt[:, 4:8], in0=stt[:, 4:8], scalar1=1.0 / 256, scalar2=EPS,
                        op0=ALU.mult, op1=ALU.add), reads=[R_stt], writes=[R_stt])
                P.act(C("activation", out=stt[:, 4:8], in_=stt[:, 4:8], func=AF.Ln), reads=[R_stt], writes=[R_stt])
                P.act(C("activation", out=stt[:, 4:8], in_=stt[:, 4:8], func=AF.Exp, scale=-0.5),
                      reads=[R_stt], writes=[R_stt])
                for h in range(4):
                    P.dve(C("scalar_tensor_tensor", out=o_[:, h * 256:(h + 1) * 256],
                            in0=o_[:, h * 256:(h + 1) * 256], scalar=stt[:, 4 + h:5 + h], in1=normgb,
                            op0=ALU.mult, op1=ALU.mult), reads=[ro, R_stt, R_normgb], writes=[ro])
                P.dve(C("tensor_tensor", out=o_, in0=o_, in1=sg_, op=ALU.mult), reads=[ro, rsg], writes=[ro])
                for half in range(2):
                    ps, rp = psum()
                    for k in range(4):
                        cc = half * 4 + k
                        P.pe(C("transpose", out=ps[:, k * 128:(k + 1) * 128], in_=o_[:, cc * 128:(cc + 1) * 128],
                               identity=ident[:]), reads=[ro, R_ident], writes=[rp])
                    for k in range(4):
                        cc = half * 4 + k
                        P.act(C("copy", out=actT[:, cc, ti * 128:(ti + 1) * 128], in_=ps[:, k * 128:(k + 1) * 128]),
                              reads=[rp], writes=[R_actT])

            cur_v = None
            state["ps_pool"] = [0, 1, 2, 3]
            for blk in p2_blocks:
                is_ctx = blk[0] < NCT
                v_ = 1 if is_ctx else 0
                nb = len(blk)
                order = list(range(nb - 1, -1, -1))
                nxtA = p2_loadA(blk[order[0]])
                if cur_v != v_:
                    P.dma("sync", gbc, modrow[l, v_:v_ + 1, 2 * D:3 * D].partition_broadcast(128), writes=[R_gbc])
                    cur_v = v_
                for ti, t in enumerate(blk):
                    P.dma("sync", xblk[:, ti, :], Xin[t * 128:(t + 1) * 128, :], writes=[R_xblk])
                prev = None
                for oi, ti in enumerate(order):
                    curA = nxtA
                    if oi + 1 < nb:
                        nxtA = p2_loadA(blk[order[oi + 1]])
                    b_, ob = p2_front(ti, blk[ti], curA)
                    if prev is not None:
                        p2_back(*prev)
                    prev = (ti, blk[ti], b_, ob)
                p2_back(*prev)
                for c in range(4):
                    wb, rw = wload(("out", l, c), wview(w_out[l])[:, :, c * 512:(c + 1) * 512])
                    for ti, t in enumerate(blk):
                        ps, rp = psum()
                        for kc in range(16):
                            P.pe(C("matmul", ps[:, :], lhsT=actT[:, kc, ti * 128:(ti + 1) * 128], rhs=wb[:, kc, :],
                                   start=(kc == 0), stop=(kc == 15)), reads=[rw, R_actT], writes=[rp])
                        P.dve(C("tensor_tensor", out=tmp5, in0=ps[:, :], in1=gbc[:, c * 512:(c + 1) * 512], op=ALU.mult),
                              reads=[rp, R_gbc], writes=[R_tmp5])
                        P.dve(C("scalar_tensor_tensor", out=xblk[:, ti, c * 512:(c + 1) * 512],
                                in0=xblk[:, ti, c * 512:(c + 1) * 512], scalar=ALPHA, in1=tmp5,
                                op0=ALU.mult, op1=ALU.add), reads=[R_xblk, R_tmp5], writes=[R_xblk])
                for ti, t in enumerate(blk):
                    xa = xblk[:, ti, :]
                    ln_stats(xa, R_xblk)
                    normalize(xa, R_xblk, xa, R_xblk)
                    P.dve(C("tensor_tensor", out=xa, in0=xa, in1=lng, op=ALU.mult), reads=[R_xblk, R_lng], writes=[R_xblk])
                    P.dve(C("tensor_tensor", out=xa, in0=xa, in1=lnb, op=ALU.add), reads=[R_xblk, R_lnb], writes=[R_xblk])
                    P.dma("sync", X1s[t * 128:(t + 1) * 128, :], xa, reads=[R_xblk])
            P.barrier()

            state["ps_pool"] = list(range(8))
            areset()
            gbc, R_gbc = aalloc("gbc3", [128, D]); lng, R_lng = aalloc("lng3", [128, D]); lnb, R_lnb = aalloc("lnb3", [128, D])
            xn3, R_xn3 = aalloc("xn3", [128, D])
            junk_holder["junk"], junk_holder["rj"] = xn3, R_xn3
            xblk, R_xblk = aalloc("xblk3", [128, TB, D])
            hidT, R_hidT = aalloc("hidT", [128, 64, 512], BF16)
            tmpr = [aalloc(f"tmpr{i}", [128, 512]) for i in range(2)]
            tmp5, R_tmp5 = aalloc("tmp53", [128, 512])
            P.dma("sync", lng, ln2_g[l:l + 1, :].partition_broadcast(128), writes=[R_lng])
            P.dma("sync", lnb, ln2_b[l:l + 1, :].partition_broadcast(128), writes=[R_lnb])
            rc = {"n": 0}

            def p3_prologue(blk):
                v_ = 1 if blk[0] < NCT else 0
                for ti, t in enumerate(blk):
                    P.dma("sync", xn3, X1s[t * 128:(t + 1) * 128, :], writes=[R_xn3])
                    ln_stats(xn3, R_xn3, junk=xblk[:, 0, :], rj=R_xblk)
                    normalize(xn3, R_xn3, xn3, R_xn3)
                    to_actT(xn3, R_xn3, ti, v_, 1)

            def p3_phaseA(blk):
                N = len(blk) * 128
                for j in range(16):
                    wb, rw = wload(("w1", l, j), wview(w1[l])[:, :, j * 512:(j + 1) * 512])
                    for fs in range(4):
                        f = j * 4 + fs
                        ps, rp = psum()
                        for kc in range(16):
                            P.pe(C("matmul", ps[:, 0:N], lhsT=wb[:, kc, fs * 128:(fs + 1) * 128], rhs=actT[:, kc, 0:N],
                                   start=(kc == 0), stop=(kc == 15)), reads=[rw, R_actT], writes=[rp])
                        tr_, rtr = tmpr[rc["n"] % 2]; rc["n"] += 1
                        P.act(C("activation", out=tr_[:, 0:N], in_=ps[:, 0:N], func=AF.Relu, bias=b1T[:, f:f + 1],
                                scale=1.0), reads=[rp, R_b1T], writes=[rtr])
                        P.dve(C("tensor_tensor", out=hidT[:, f, 0:N], in0=tr_[:, 0:N], in1=tr_[:, 0:N], op=ALU.mult),
                              reads=[rtr], writes=[R_hidT])

            def p3_phaseB(blk):
                nb = len(blk)
                for ti, t in enumerate(blk):
                    P.dma("sync", xblk[:, ti, :], X1s[t * 128:(t + 1) * 128, :], writes=[R_xblk])
                for c in range(4):
                    acc = [psum() for _ in range(nb)]
                    for ti in range(nb):
                        ps, rp = acc[ti]
                        P.pe(C("matmul", ps[:, :], lhsT=oneh[0:4, c, :], rhs=b2r[0:4, :], start=True, stop=False),
                             reads=[R_oneh, R_b2r], writes=[rp])
                    for g in range(4):
                        wb, rw = wload(("w2", l, c, g),
                                       w2[l].rearrange("(g fi p) n -> g p fi n", fi=16, p=128)[g][:, :, c * 512:(c + 1) * 512])
                        for fi in range(16):
                            f = g * 16 + fi
                            for ti in range(nb):
                                ps, rp = acc[ti]
                                P.pe(C("matmul", ps[:, :], lhsT=hidT[:, f, ti * 128:(ti + 1) * 128], rhs=wb[:, fi, :],
                                       start=False, stop=(g == 3 and fi == 15)), reads=[rw, R_hidT], writes=[rp])
                    for ti in range(nb):
                        ps, rp = acc[ti]
                        P.dve(C("tensor_tensor", out=tmp5, in0=ps[:, :], in1=gbc[:, c * 512:(c + 1) * 512], op=ALU.mult),
                              reads=[rp, R_gbc], writes=[R_tmp5])
                        P.dve(C("scalar_tensor_tensor", out=xblk[:, ti, c * 512:(c + 1) * 512],
                                in0=xblk[:, ti, c * 512:(c + 1) * 512], scalar=ALPHA, in1=tmp5,
                                op0=ALU.mult, op1=ALU.add), reads=[R_xblk, R_tmp5], writes=[R_xblk])

            def p3_final(blk):
                for ti, t in enumerate(blk):
                    xa = xblk[:, ti, :]
                    ln_stats(xa, R_xblk, junk=xn3, rj=R_xn3)
                    normalize(xa, R_xblk, xa, R_xblk)
                    P.dve(C("tensor_tensor", out=xa, in0=xa, in1=lng, op=ALU.mult), reads=[R_xblk, R_lng], writes=[R_xblk])
                    P.dve(C("tensor_tensor", out=xa, in0=xa, in1=lnb, op=ALU.add), reads=[R_xblk, R_lnb], writes=[R_xblk])
                    if last:
                        P.dma("sync", y[(t - NCT) * 128:(t - NCT + 1) * 128, :], xa, reads=[R_xblk])
                    else:
                        P.dma("sync", X2s[t * 128:(t + 1) * 128, :], xa, reads=[R_xblk])

            cur_v = None
            p3_prologue(p3_blocks[0])
            for bi_, blk in enumerate(p3_blocks):
                v_ = 1 if blk[0] < NCT else 0
                p3_phaseA(blk)
                if bi_ + 1 < len(p3_blocks):
                    p3_prologue(p3_blocks[bi_ + 1])
                if cur_v != v_:
                    P.dma("sync", gbc, modrow[l, v_:v_ + 1, 5 * D:6 * D].partition_broadcast(128), writes=[R_gbc])
                    cur_v = v_
                p3_phaseB(blk)
                p3_final(blk)
            P.barrier()
        P.emit(st)
    return nc


def _pool_mats():
    wins = (2, 4, 8, 16)

    def A(seg_len, w):
        pos = np.arange(seg_len)
        lo = np.maximum(pos - w // 2, 0)
        hi = np.minimum(pos + w // 2 - 1, seg_len - 1)
        M = np.zeros((seg_len, seg_len), np.float32)
        for i in range(seg_len):
            M[i, lo[i]:hi[i] + 1] = 1.0 / float(hi[i] - lo[i] + 1)
            M[i, i] -= 1.0
        return M
    poolL = np.zeros((128, 4, 128), np.float32)
    poolC = np.zeros((128, 16, 128), np.float32)
    for gi, w in enumerate(wins):
        a64 = A(64, w)
        full = np.zeros((128, 128), np.float32)
        full[:64, :64] = a64
        full[64:, 64:] = a64
        poolL[:, gi, :] = full.T
        a256 = A(256, w)
        for tj in range(2):
            for ti in range(2):
                poolC[:, gi * 4 + tj * 2 + ti, :] = a256[ti * 128:(ti + 1) * 128, tj * 128:(tj + 1) * 128].T
    return poolL, poolC


def _consts():
    j = np.arange(128)[:, None]
    i = np.arange(128)[None, :]
    s = np.float32(-1.0 / 16.0)
    tri = np.zeros((128, 4, 128), np.float32)
    tri[:, 0, :] = (j <= i) * s
    tri[:, 1, :] = (j > i) * s
    tri[:, 2, :] = (j >= i) * s
    tri[:, 3, :] = (j < i) * s
    mask = np.zeros((128, 2, 128), np.float32)
    mask[:, 0, :] = (j <= i)
    mask[:, 1, :] = (j >= i)
    oneh = np.zeros((4, 4, 128), np.float32)
    for k in range(4):
        oneh[k, k, :] = 1.0
    poolL, poolC = _pool_mats()
    return dict(c_ident=np.eye(128, dtype=np.float32), c_tri=tri, c_mask=mask, c_poolL=poolL, c_poolC=poolC,
                c_oneh=oneh)


_NC_CACHE = {}


def kernel(x, c, ctx, c_ctx, w_ada, b_ada, w_in, w_gate_up, b_gate, gla_norm_g, w_pool, pool_scale, w_out,
           ln1_g, ln1_b, w_mlp1, b_mlp1, w_mlp2, b_mlp2, ln2_g, ln2_b):
    f = lambda a: np.ascontiguousarray(np.asarray(a, dtype=np.float32))
    x = f(x); c = f(c); ctx = f(ctx); c_ctx = f(c_ctx); w_in = f(w_in)
    z16 = np.zeros((L, D, 16), np.float32)
    kc_, vc_, lrf, lrb = w_in[:, :, 0:512], w_in[:, :, 512:1536], w_in[:, :, 1536:1552], w_in[:, :, 1552:1568]
    q_, g_, u_ = w_in[:, :, 1568:2080], w_in[:, :, 2080:3104], w_in[:, :, 3104:4128]
    w_tm = np.ascontiguousarray(np.concatenate([kc_, vc_, g_, u_], axis=2))
    shared = dict(
        w_ada=f(w_ada), b_ada=f(b_ada), w_tm=w_tm, normg=f(gla_norm_g), w_pool=f(w_pool), pool_scale=f(pool_scale),
        w_out=f(w_out), ln1_g=f(ln1_g), ln1_b=f(ln1_b), w1=f(w_mlp1), b1=f(b_mlp1), w2=f(w_mlp2), b2=f(b_mlp2),
        ln2_g=f(ln2_g), ln2_b=f(ln2_b))
    in_maps = []
    T = NLT * 128
    consts = _consts()
    wg = f(w_gate_up); bg = f(b_gate)
    per_par = []
    for par in range(2):
        lrA, lrB = (lrf, lrb) if par == 0 else (lrb, lrf)
        d = dict(shared)
        d["w_fm"] = np.ascontiguousarray(np.concatenate([q_, kc_, lrA, z16, lrB, z16], axis=2))
        d["w_gate"] = np.ascontiguousarray(wg if par == 0 else wg[:, ::-1])
        d["b_gate"] = np.ascontiguousarray((bg if par == 0 else bg[:, ::-1]).reshape(L, 1, 1024))
        d.update(consts)
        if par == 1:
            d["c_poolL"] = np.ascontiguousarray(consts["c_poolL"][::-1, :, ::-1])
            pc = consts["c_poolC"].reshape(128, 4, 2, 2, 128)
            d["c_poolC"] = np.ascontiguousarray(pc[::-1, :, ::-1, ::-1, ::-1].reshape(128, 16, 128))
        per_par.append(d)
    for core in range(N_CORES):
        b = core // 2 if SPLIT else core % 4
        par = core % 2 if SPLIT else 0
        m = dict(per_par[par])
        xb = x[b][:T]
        cb = ctx[b]
        if par == 1:
            xb = xb[::-1]
            cb = cb[::-1]
        m["xin"] = np.ascontiguousarray(np.concatenate([cb, xb], axis=0))
        m["cvec"] = np.ascontiguousarray(np.stack([c[b], c_ctx], axis=0).reshape(32, 128))
        in_maps.append(m)
    if "nc" not in _NC_CACHE:
        _NC_CACHE["nc"] = build_nc()
    res = run_bass_kernel_spmd(_NC_CACHE["nc"], in_maps, core_ids=list(range(N_CORES)))
    kernel.last_results = res.results
    nb_out = N_CORES // 2 if SPLIT else min(4, N_CORES)
    out = np.zeros((nb_out, T, D), np.float32)
    if SPLIT:
        for core in range(N_CORES):
            yb = res.results[core]["y"].reshape(T // 2, D)
            if core % 2 == 0:
                out[core // 2, 0:T // 2] = yb
            else:
                out[core // 2, T // 2:T] = yb[::-1]
    else:
        for b in range(nb_out):
            out[b] = res.results[b]["y"].reshape(T, D)
    return out.astype(np.float32)
```

```python
import numpy as np
from contextlib import ExitStack
import concourse.bass as bass
import concourse.mybir as mybir
from concourse.bass_utils import run_bass_kernel_spmd

F32 = mybir.dt.float32
BF16 = mybir.dt.bfloat16
ALU = mybir.AluOpType
AF = mybir.ActivationFunctionType
AX = mybir.AxisListType

D = 2048
L = 2
NCT = 2
NLT = 32
NT = NCT + NLT
TB = 4
DFF = 8192
ALPHA = float((2 * L) ** 0.25)
QSCALE = float(128 ** -0.5)
EPS = 1e-6
N_CORES = 8
WCACHE = True
SPLIT = True
DEBUG = False
L_RUN = L


def C(name, *a, **k):
    return lambda e: getattr(e, name)(*a, **k)


class Res:
    __slots__ = ("name", "last_w", "readers", "const")

    def __init__(self, name, const=False):
        self.name = name
        self.last_w = None
        self.readers = []
        self.const = const


class Op:
    __slots__ = ("eng", "fn", "deps", "dma", "sem", "cnt", "awaited", "name")


class Prog:
    ENGS = ["tensor", "vector", "scalar", "gpsimd", "sync"]

    def __init__(self, nc, n_dma_sems=(40, 40)):
        self.nc = nc
        self.ops = {e: [] for e in self.ENGS}
        self.n_dma_sems = {"sync": n_dma_sems[0], "gpsimd": n_dma_sems[1]}
        self.dmas_since_barrier = []

    def add(self, eng, fn, reads=(), writes=(), dma=False, name=""):
        op = Op()
        op.eng = eng; op.fn = fn; op.dma = dma; op.awaited = False; op.name = name
        op.sem = None; op.cnt = 0
        deps = set()
        for r in reads:
            if r.last_w is not None:
                deps.add(r.last_w)
            if not r.const:
                r.readers.append(op)
        for r in writes:
            deps.update(r.readers)
            if r.last_w is not None:
                deps.add(r.last_w)
            r.readers = []
            r.last_w = op
        deps.discard(op)
        op.deps = deps
        for d in deps:
            d.awaited = True
        self.ops[eng].append(op)
        if dma:
            self.dmas_since_barrier.append(op)
        return op

    def pe(self, fn, reads=(), writes=()):
        return self.add("tensor", fn, reads, writes)

    def dve(self, fn, reads=(), writes=()):
        return self.add("vector", fn, reads, writes)

    def act(self, fn, reads=(), writes=()):
        return self.add("scalar", fn, reads, writes)

    def dma(self, q, out, in_, reads=(), writes=()):
        return self.add(q, C("dma_start", out=out, in_=in_), reads, writes, dma=True)

    def barrier(self):
        lasts = [self.ops[e][-1] for e in ["tensor", "vector", "scalar"] if self.ops[e]]
        lasts += [o for o in self.ops["gpsimd"][-1:] if not o.dma]
        deps = set(lasts) | set(self.dmas_since_barrier)
        self.dmas_since_barrier = []
        for e in self.ENGS:
            op = self.add(e, lambda g: g.nop())
            op.deps |= {d for d in deps if d is not op}
            for d in op.deps:
                d.awaited = True

    def emit(self, stack):
        nc = self.nc
        esem = {}
        for e in ["tensor", "vector", "scalar", "gpsimd"]:
            esem[e] = stack.enter_context(nc.semaphore("s_" + e))
        dsem = {q: [stack.enter_context(nc.semaphore(f"d_{q}_{i}")) for i in range(n)]
                for q, n in self.n_dma_sems.items()}
        for e in self.ENGS:
            c = 0
            rr = 0
            use = {}
            prev_user = {}
            for op in self.ops[e]:
                if op.dma:
                    pool = dsem[e]
                    k = rr % len(pool)
                    rr += 1
                    use[k] = use.get(k, 0) + 1
                    op.sem = pool[k]
                    op.cnt = 16 * use[k]
                    if k in prev_user:
                        op.deps.add(prev_user[k])
                    prev_user[k] = op
                    op.awaited = True
                else:
                    if e == "sync":
                        continue
                    if op.awaited:
                        c += 1
                    op.sem = esem[e]
                    op.cnt = c
        block = stack.enter_context(nc.Block())

        def make(e):
            def body(eng):
                known = {}
                for op in self.ops[e]:
                    need = {}
                    for d in op.deps:
                        if d.sem is None:
                            continue
                        if (not d.dma) and d.eng == e and e == "tensor":
                            continue
                        key = id(d.sem)
                        if key not in need or need[key][1] < d.cnt:
                            need[key] = (d.sem, d.cnt)
                    for key, (s, cnt) in need.items():
                        if known.get(key, 0) >= cnt:
                            continue
                        eng.wait_ge(s, cnt)
                        known[key] = cnt
                    ins = op.fn(eng)
                    if op.dma:
                        ins.then_inc(op.sem, 16)
                    elif op.awaited and op.sem is not None:
                        ins.then_inc(op.sem, 1)
            return body

        block.tensor(make("tensor"))
        block.vector(make("vector"))
        block.scalar(make("scalar"))
        block.gpsimd(make("gpsimd"))
        block.sync(make("sync"))


def build_nc():
    nc = bass.Bass("TRN2", target_bir_lowering=False)
    P = Prog(nc)

    def din(name, shape, dt=F32):
        return nc.dram_tensor(name, list(shape), dt, kind="ExternalInput").ap()

    def dscr(name, shape=None, dt=F32, dbg=False):
        if dbg and DEBUG:
            return nc.dram_tensor(name, list(shape), dt, kind="ExternalOutput").ap()
        return nc.dram_tensor(name, list(shape), dt).ap()

    xin = din("xin", [NT * 128, D])
    cvec = din("cvec", [32, 128])
    w_ada = din("w_ada", [L, D, 6 * D])
    b_ada = din("b_ada", [L, 6 * D])
    w_fm = din("w_fm", [L, D, 1088])
    w_tm = din("w_tm", [L, D, 3584])
    w_gate = din("w_gate", [L, 2, 16, 512])
    b_gate = din("b_gate", [L, 1, 1024])
    normg = din("normg", [L, 256])
    w_pool = din("w_pool", [L, 4, 256, 256])
    pool_scale = din("pool_scale", [L, 1024])
    w_out = din("w_out", [L, D, D])
    ln1_g = din("ln1_g", [L, D]); ln1_b = din("ln1_b", [L, D])
    w1 = din("w1", [L, D, DFF]); b1 = din("b1", [L, DFF])
    w2 = din("w2", [L, DFF, D]); b2 = din("b2", [L, D])
    ln2_g = din("ln2_g", [L, D]); ln2_b = din("ln2_b", [L, D])
    c_ident = din("c_ident", [128, 128])
    c_tri = din("c_tri", [128, 4, 128])
    c_mask = din("c_mask", [128, 2, 128])
    c_poolL = din("c_poolL", [128, 4, 128])
    c_poolC = din("c_poolC", [128, 16, 128])
    c_oneh = din("c_oneh", [4, 4, 128])
    NOWN = NLT // 2 if SPLIT else NLT
    y = nc.dram_tensor("y", [NOWN * 128, D], F32, kind="ExternalOutput").ap()

    X1s = dscr("X1s", [NT * 128, D], dbg=True)
    X2s = dscr("X2s", [NT * 128, D], dbg=True)
    modrow = dscr("modrow", [L, 2, 6 * D], dbg=True)
    st_qt = dscr("st_qt", dbg=True, shape=[NT, 128, 512], dt=BF16)
    st_kt = dscr("st_kt", dbg=True, shape=[NT, 128, 512], dt=BF16)
    st_kh = dscr("st_kh", dbg=True, shape=[NT, 128, 512], dt=BF16)
    st_et = dscr("st_et", dbg=True, shape=[NT, 128, 4])
    st_v = dscr("st_v", dbg=True, shape=[NT, 128, 1024], dt=BF16)
    st_sg = dscr("st_sg", dbg=True, shape=[NT, 128, 1024])
    st_oA = dscr("st_oA", dbg=True, shape=[NT, 128, 1024])
    st_mp = dscr("st_mp", dbg=True, shape=[NT, 128, 1024], dt=BF16)
    wscr = dscr("wscr", shape=[96, 128, 16 * 512], dt=BF16) if WCACHE else None
    dbg_mix = dscr("dbg_mix", dbg=True, shape=[NT, 128, 1024])
    dbg_o = dscr("dbg_o", dbg=True, shape=[NT, 128, 1024])
    dbg_r = dscr("dbg_r", dbg=True, shape=[NT * 128, D])

    with ExitStack() as st:
        def sb(name, shape, dt=F32):
            return st.enter_context(nc.sbuf_tensor(name, list(shape), dt))

        ident = sb("ident", [128, 128]); R_ident = Res("ident", const=True)
        ones1 = sb("ones1", [1, 128]); R_ones1 = Res("ones1", const=True)
        oneh = sb("oneh", [4, 4, 128]); R_oneh = Res("oneh", const=True)
        cv = sb("cv", [32, 128]); R_cv = Res("cv")
        silT = sb("silT", [128, 16, 2], BF16); R_silT = Res("silT")
        modT = sb("modT", [128, 2, 96]); R_modT = Res("modT")
        opsc = sb("opsc", [128, 2, 2, 16]); R_opsc = Res("opsc")
        m96 = sb("m96", [96, 128]); R_m96 = Res("m96")
        b1T = sb("b1T", [128, 64]); R_b1T = Res("b1T")
        b1l = sb("b1l", [64, 128]); R_b1l = Res("b1l")
        b2r = sb("b2r", [4, 512]); R_b2r = Res("b2r")
        poolsT = sb("poolsT", [128, 8]); R_poolsT = Res("poolsT")
        psl = sb("psl", [8, 128]); R_psl = Res("psl")
        badd = [sb(f"badd{i}", [2, 512]) for i in range(2)]; R_badd = [Res(f"badd{i}") for i in range(2)]
        mrow = [sb(f"mrow{i}", [2, 512]) for i in range(2)]; R_mrow = [Res(f"mrow{i}") for i in range(2)]
        actT = sb("actT", [128, 16, 512], BF16); R_actT = Res("actT")
        wbufs = [sb(f"wbuf{i}", [128, 16, 512], BF16) for i in range(2)]
        R_wbufs = [Res(f"wbuf{i}") for i in range(2)]
        sttN = sb("sttN", [128, 8, 8])
        stsets = [(sttN[:, i, :], Res(f"stt{i}")) for i in range(8)]
        stcur = {"n": 0, "st": stsets[0]}
        ARENA_BYTES = 136 * 1024
        arena = sb("arena", [128, ARENA_BYTES // 2], dt=BF16)
        psb = [st.enter_context(nc.psum_tensor(f"psb{i}", [128, 512], F32)) for i in range(8)]
        R_ps = [Res(f"ps{i}") for i in range(8)]

        state = {"ps": 0, "wb": 0, "aoff": 0, "ps_pool": list(range(8))}

        def psum():
            pool = state["ps_pool"]
            i = pool[state["ps"] % len(pool)]
            state["ps"] += 1
            return psb[i], R_ps[i]

        def wbuf():
            i = state["wb"] % 2
            state["wb"] += 1
            return wbufs[i], R_wbufs[i]

        def aalloc(name, shape, dt=F32):
            esz = 4 if dt == F32 else 2
            n = 1
            for s in shape[1:]:
                n *= s
            nbytes = n * esz
            off = state["aoff"]
            off = (off + 63) // 64 * 64
            assert off + nbytes <= ARENA_BYTES, (name, off, nbytes)
            state["aoff"] = off + nbytes
            v = arena[0:shape[0], off // 2:(off + nbytes) // 2]
            if dt == F32:
                v = v.bitcast(F32)
            if len(shape) == 3:
                v = v.rearrange("p (a b) -> p a b", b=shape[2])
            return v, Res(name)

        def areset(mark=0):
            state["aoff"] = mark

        wcache = {}

        def wload(key, src_ap, ncol=512):
            wb, rw = wbuf()
            if key is None or not WCACHE:
                P.dma("gpsimd", wb[:, :, 0:ncol], src_ap, writes=[rw])
            elif key in wcache:
                idx, rres = wcache[key]
                P.dma("gpsimd", wb[:, :, 0:ncol], wscr[idx].rearrange("p (a b) -> p a b", b=512)[:, :, 0:ncol], reads=[rres], writes=[rw])
            else:
                idx = len(wcache)
                rres = Res(f"wscr{idx}")
                wcache[key] = (idx, rres)
                P.dma("gpsimd", wb[:, :, 0:ncol], src_ap, writes=[rw])
                P.dma("sync", wscr[idx].rearrange("p (a b) -> p a b", b=512)[:, :, 0:ncol], wb[:, :, 0:ncol], reads=[rw], writes=[rres])
            return wb, rw

        def wview(w2d):
            return w2d.rearrange("(kc p) n -> p kc n", p=128)

        P.dma("sync", ident[:], c_ident[:, :], writes=[R_ident])
        P.dma("sync", oneh[:], c_oneh[:, :, :], writes=[R_oneh])
        P.dve(C("memset", ones1[:], 1.0), writes=[R_ones1])
        P.dma("sync", cv[:], cvec[:, :], writes=[R_cv])
        P.act(C("activation", out=cv[:], in_=cv[:], func=AF.Silu), reads=[R_cv], writes=[R_cv])
        ps, rp = psum()
        P.pe(C("transpose", out=ps[:, 0:32], in_=cv[0:32, :], identity=ident[0:32, 0:32]),
             reads=[R_cv, R_ident], writes=[rp])
        for v in range(2):
            P.dve(C("tensor_copy", out=silT[:, :, v], in_=ps[:, v * 16:(v + 1) * 16]),
                  reads=[rp], writes=[R_silT])

        def next_stats():
            stcur["n"] += 1
            stcur["st"] = stsets[stcur["n"] % 8]
            return stcur["st"]

        def ln_stats(x_ap, rx, junk=None, rj=None, act_sum=False):
            if junk is None:
                junk, rj = aalloc_junk()
            stt, R_stt = next_stats()
            P.dve(C("memset", stt[:, 0:2], 0.0), writes=[R_stt])
            if act_sum:
                P.act(C("activation", out=junk, in_=x_ap, func=AF.Identity, accum_out=stt[:, 0:1]),
                      reads=[rx], writes=[rj, R_stt])
            else:
                P.dve(C("reduce_sum", out=stt[:, 0:1], in_=x_ap, axis=AX.X), reads=[rx], writes=[R_stt])
            P.act(C("activation", out=junk, in_=x_ap, func=AF.Square, accum_out=stt[:, 1:2]),
                  reads=[rx], writes=[rj, R_stt])
            P.dve(C("tensor_scalar", out=stt[:, 0:1], in0=stt[:, 0:1], scalar1=1.0 / D, scalar2=None,
                    op0=ALU.mult), reads=[R_stt], writes=[R_stt])
            P.dve(C("tensor_tensor", out=stt[:, 2:3], in0=stt[:, 0:1], in1=stt[:, 0:1], op=ALU.mult),
                  reads=[R_stt], writes=[R_stt])
            P.dve(C("scalar_tensor_tensor", out=stt[:, 3:4], in0=stt[:, 1:2], scalar=1.0 / D,
                    in1=stt[:, 2:3], op0=ALU.mult, op1=ALU.subtract), reads=[R_stt], writes=[R_stt])
            P.dve(C("tensor_scalar", out=stt[:, 3:4], in0=stt[:, 3:4], scalar1=EPS, scalar2=None,
                    op0=ALU.add), reads=[R_stt], writes=[R_stt])
            P.act(C("activation", out=stt[:, 3:4], in_=stt[:, 3:4], func=AF.Ln), reads=[R_stt], writes=[R_stt])
            P.act(C("activation", out=stt[:, 3:4], in_=stt[:, 3:4], func=AF.Exp, scale=-0.5),
                  reads=[R_stt], writes=[R_stt])

        junk_holder = {}

        def aalloc_junk():
            return junk_holder["junk"], junk_holder["rj"]

        def ln_affine(xa, rx, g_ap, rg, b_ap, rb):
            stt, R_stt = stcur["st"]
            P.dve(C("scalar_tensor_tensor", out=xa, in0=xa, scalar=stt[:, 0:1], in1=g_ap, op0=ALU.subtract,
                    op1=ALU.mult), reads=[rx, R_stt, rg], writes=[rx])
            P.dve(C("scalar_tensor_tensor", out=xa, in0=xa, scalar=stt[:, 3:4], in1=b_ap, op0=ALU.mult,
                    op1=ALU.add), reads=[rx, R_stt, rb], writes=[rx])

        def normalize(out_ap, rout, x_ap, rx):
            stt, R_stt = stcur["st"]
            P.dve(C("tensor_scalar", out=out_ap, in0=x_ap, scalar1=stt[:, 0:1], scalar2=stt[:, 3:4],
                    op0=ALU.subtract, op1=ALU.mult), reads=[rx, R_stt], writes=[rout])

        def to_actT(xn_ap, rxn, ti, v, which):
            for q4 in range(4):
                ps, rp = psum()
                for j in range(4):
                    dc = q4 * 4 + j
                    P.pe(C("transpose", out=ps[:, j * 128:(j + 1) * 128],
                                                                 in_=xn_ap[:, dc * 128:(dc + 1) * 128],
                                                                 identity=ident[:]),
                         reads=[rxn, R_ident], writes=[rp])
                for j in range(4):
                    dc = q4 * 4 + j
                    P.act(C("activation",
                        out=actT[:, dc, ti * 128:(ti + 1) * 128], in_=ps[:, j * 128:(j + 1) * 128],
                        func=AF.Identity, bias=modT[:, v, (3 * which) * 16 + dc:(3 * which) * 16 + dc + 1],
                        scale=opsc[:, v, which, dc:dc + 1]),
                        reads=[rp, R_modT, R_opsc], writes=[R_actT])

        def vec_cols(dst, rdst, tmp, rtmp, src_2d, nrow):
            P.dma("sync", tmp[0:nrow, :], src_2d, writes=[rtmp])
            ps, rp = psum()
            P.pe(C("transpose", out=ps[:, 0:nrow], in_=tmp[0:nrow, :], identity=ident[0:nrow, 0:nrow]),
                 reads=[rtmp, R_ident], writes=[rp])
            P.dve(C("tensor_copy", out=dst, in_=ps[:, 0:nrow]), reads=[rp], writes=[rdst])

        def scan_tile(qt, kt, kh, et, v_ap, S, Sb, PT, mask, rin, rS, rSb, rPT, rmask, ofix=None):
            for h in range(4):
                ps, rp = psum()
                P.pe(C("matmul", ps[:, 0:128], lhsT=kt[:, h, :], rhs=qt[:, h, :],
                                                    start=True, stop=True), reads=rin, writes=[rp])
                P.dve(C("tensor_tensor", out=PT[:, h, :], in0=ps[:, 0:128], in1=mask,
                                                            op=ALU.mult), reads=[rp, rmask], writes=[rPT])
            obanks = []
            for hp in range(2):
                ps, rp = psum() if ofix is None else (psb[ofix[hp]], R_ps[ofix[hp]])
                for hh in range(2):
                    h = hp * 2 + hh
                    P.pe(C("matmul", ps[:, hh * 256:(hh + 1) * 256], lhsT=PT[:, h, :],
                                                               rhs=v_ap[:, h * 256:(h + 1) * 256],
                                                               start=True, stop=False),
                         reads=rin + [rPT], writes=[rp])
                    P.pe(C("matmul", ps[:, hh * 256:(hh + 1) * 256], lhsT=qt[:, h, :],
                                                               rhs=Sb[:, h, :], start=False, stop=True),
                         reads=rin + [rSb], writes=[rp])
                obanks.append((ps, rp))
            state_update(kh, et, v_ap, S, Sb, rin, rS, rSb)
            return obanks

        def state_update(kh, et, v_ap, S, Sb, rin, rS, rSb):
            for hp in range(2):
                ps, rp = psum()
                for hh in range(2):
                    h = hp * 2 + hh
                    P.pe(C("matmul", ps[:, hh * 256:(hh + 1) * 256],
                                                               lhsT=kh[:, h * 128:(h + 1) * 128],
                                                               rhs=v_ap[:, h * 256:(h + 1) * 256],
                                                               start=True, stop=True), reads=rin, writes=[rp])
                for hh in range(2):
                    h = hp * 2 + hh
                    P.dve(C("scalar_tensor_tensor",
                        out=S[:, h, :], in0=S[:, h, :], scalar=et[:, h:h + 1], in1=ps[:, hh * 256:(hh + 1) * 256],
                        op0=ALU.mult, op1=ALU.add), reads=[rp, rS] + rin, writes=[rS])
                    P.act(C("copy", out=Sb[:, h, :], in_=S[:, h, :]), reads=[rS], writes=[rSb])

        blocks_ctx = [list(range(0, NCT))]
        blocks_lat = [list(range(NCT + b * TB, NCT + (b + 1) * TB)) for b in range(NLT // TB)]
        n_own = (NLT // TB) // 2 if SPLIT else NLT // TB
        blocks_own = blocks_lat[:n_own]
        blocks_oth = blocks_lat[n_own:]

        for l in range(L_RUN):
            last = (l == L - 1)
            Xin = xin if l == 0 else X2s
            areset()
            for j in range(24):
                wb, rw = wload(None, wview(w_ada[l])[:, :, j * 512:(j + 1) * 512])
                bi = j % 2
                P.dma("sync", badd[bi][:], b_ada[l:l + 1, j * 512:(j + 1) * 512].partition_broadcast(2),
                      writes=[R_badd[bi]])
                ps, rp = psum()
                for kc in range(16):
                    P.pe(C("matmul", ps[0:2, :], lhsT=silT[:, kc, :], rhs=wb[:, kc, :],
                           start=(kc == 0), stop=(kc == 15)), reads=[R_silT, rw], writes=[rp])
                P.dve(C("tensor_tensor", out=mrow[bi][:], in0=ps[0:2, :], in1=badd[bi][:], op=ALU.add),
                      reads=[rp, R_badd[bi]], writes=[R_mrow[bi]])
                P.dma("sync", modrow[l, :, j * 512:(j + 1) * 512], mrow[bi][:], reads=[R_mrow[bi]])
            P.barrier()
            for v in range(2):
                vec_cols(modT[:, v, :], R_modT, m96, R_m96,
                         modrow[l, v, :].rearrange("(r p) -> r p", p=128), 96)
            for v in range(2):
                for which in range(2):
                    P.dve(C("tensor_scalar", out=opsc[:, v, which, :],
                            in0=modT[:, v, (3 * which + 1) * 16:(3 * which + 2) * 16],
                            scalar1=1.0, scalar2=None, op0=ALU.add), reads=[R_modT], writes=[R_opsc])
            vec_cols(b1T[:], R_b1T, b1l, R_b1l, b1[l, :].rearrange("(r p) -> r p", p=128), 64)
            vec_cols(poolsT[:], R_poolsT, psl, R_psl, pool_scale[l, :].rearrange("(r p) -> r p", p=128), 8)
            P.dma("sync", b2r[:], b2[l, :].rearrange("(r n) -> r n", n=512), writes=[R_b2r])
            P.barrier()

            areset()
            tri, R_tri = aalloc("tri", [128, 4, 128]); R_tri.const = True
            msk, R_msk = aalloc("msk", [128, 2, 128]); R_msk.const = True
            poolL, R_poolL = aalloc("poolL", [128, 4, 128]); R_poolL.const = True
            poolC, R_poolC = aalloc("poolC", [128, 16, 128]); R_poolC.const = True
            wpool, R_wpool = aalloc("wpool", [128, 8, 256], BF16); R_wpool.const = True
            wgA, R_wgate = aalloc("wgA", [16, 512]); R_wgate.const = True
            wgB, _ = aalloc("wgB", [16, 512])
            wgd = [wgA, wgB]
            bgate, R_bgate = aalloc("bgate", [1, 1024]); R_bgate.const = True
            normgb, R_normgb = aalloc("normgb", [128, 256]); R_normgb.const = True
            SA, R_SA = aalloc("SA", [128, 4, 256]); SAb, R_SAb = aalloc("SAb", [128, 4, 256], BF16)
            SB, R_SB = aalloc("SB", [128, 4, 256]); SBb, R_SBb = aalloc("SBb", [128, 4, 256], BF16)
            mark12 = state["aoff"]
            P.dma("sync", tri, c_tri[:, :, :], writes=[R_tri])
            P.dma("sync", msk, c_mask[:, :, :], writes=[R_msk])
            P.dma("sync", poolL, c_poolL[:, :, :], writes=[R_poolL])
            P.dma("sync", poolC, c_poolC[:, :, :], writes=[R_poolC])
            P.dma("gpsimd", wpool.rearrange("p (g c) d -> p g c d", c=2),
                  w_pool[l].rearrange("g (c p) d -> p g c d", p=128), writes=[R_wpool])
            P.dma("sync", wgA, w_gate[l, 0, :, :], writes=[R_wgate])
            P.dma("sync", wgB, w_gate[l, 1, :, :], writes=[R_wgate])
            P.dma("sync", bgate, b_gate[l, :, :], writes=[R_bgate])
            P.dma("sync", normgb, normg[l:l + 1, :].partition_broadcast(128), writes=[R_normgb])
            for S_, r_ in ((SA, R_SA), (SB, R_SB), (SAb, R_SAb), (SBb, R_SBb)):
                P.dve(C("memset", S_, 0.0), writes=[r_])

            areset(mark12)
            xt = [aalloc(f"xt{i}", [128, D]) for i in range(2)]
            junk_holder["junk"], junk_holder["rj"] = aalloc("junk", [128, D])
            qT, R_qT = aalloc("qT", [128, 4, 512], BF16)
            kT, R_kT = aalloc("kT", [128, 4, 512], BF16)
            lrA, R_lrT = aalloc("lrA", [16, 512])
            lrB, _ = aalloc("lrB", [16, 512])
            lrd = [lrA, lrB]
            ktm, R_ktm = aalloc("ktm", [128, TB, 512], BF16)
            vb, R_vb = aalloc("vb", [128, TB, 1024], BF16)
            ub, R_ub = aalloc("ub", [128, TB, 1024])
            sgr = [aalloc(f"sgr{i}", [128, 512]) for i in range(2)]
            tmpg, R_tmpg = aalloc("tmpg", [128, 512])
            spd = [aalloc(f"spd{i}", [128, 512]) for i in range(2)]
            ecum, R_ecum = aalloc("ecum", [128, 4, 128])
            einv, R_einv = aalloc("einv", [128, 4, 128])
            erev, R_erev = aalloc("erev", [128, 512])
            qtd = [aalloc(f"qtd{i}", [128, 4, 128], BF16) for i in range(2)]
            ktd = [aalloc(f"ktd{i}", [128, 4, 128], BF16) for i in range(2)]
            khd = [aalloc(f"khd{i}", [128, 512], BF16) for i in range(2)]
            etd = [aalloc(f"etd{i}", [128, 4]) for i in range(2)]
            PT, R_PT = aalloc("PT", [128, 4, 128], BF16)
            oA, R_oA = aalloc("oA", [128, 1024])
            residT, R_residT = aalloc("residT", [128, 8, 128], BF16)
            mp, R_mp = aalloc("mp", [128, 8, 128], BF16)
            cnt = {"x": 0, "sg": 0}

            def gates(ti, d, full):
                gates_pre(ti, d)
                gates_post(ti, d, full)

            def gates_pre(ti, d):
                tsl = slice(ti * 128, (ti + 1) * 128)
                ps, rp = psum()
                P.pe(C("matmul", ps[:, :], lhsT=lrd[d][0:16, tsl], rhs=wgd[d][0:16, :],
                       start=True, stop=False), reads=[R_lrT, R_wgate], writes=[rp])
                P.pe(C("matmul", ps[:, :], lhsT=ones1[0:1, :], rhs=bgate[0:1, d * 512:(d + 1) * 512],
                       start=False, stop=True), reads=[R_ones1, R_bgate], writes=[rp])
                sp_, rsp = spd[d]
                P.act(C("activation", out=tmpg, in_=ps[:, :], func=AF.Exp, scale=-1.0), reads=[rp], writes=[R_tmpg])
                P.act(C("activation", out=sp_, in_=tmpg, func=AF.Ln, bias=1.0, scale=1.0),
                      reads=[R_tmpg], writes=[rsp])

            def gates_post(ti, d, full):
                tsl = slice(ti * 128, (ti + 1) * 128)
                sp_, rsp = spd[d]
                ps, rp = psum()
                for h in range(4):
                    P.pe(C("matmul", ps[:, h * 128:(h + 1) * 128], lhsT=sp_[:, h * 128:(h + 1) * 128],
                           rhs=tri[:, 2 * d, :], start=True, stop=True), reads=[rsp, R_tri], writes=[rp])
                P.act(C("activation", out=ecum.rearrange("p a b -> p (a b)"), in_=ps[:, :], func=AF.Exp),
                      reads=[rp], writes=[R_ecum])
                qt_, rqt = qtd[d]; kt_, rkt = ktd[d]; kh_, rkh = khd[d]; et_, ret = etd[d]
                if full:
                    P.act(C("activation", out=einv.rearrange("p a b -> p (a b)"), in_=ps[:, :], func=AF.Exp,
                            scale=-1.0), reads=[rp], writes=[R_einv])
                    P.dve(C("tensor_tensor", out=qt_, in0=qT[:, :, tsl], in1=ecum, op=ALU.mult),
                          reads=[R_qT, R_ecum], writes=[rqt])
                    P.dve(C("tensor_tensor", out=kt_, in0=kT[:, :, tsl], in1=einv, op=ALU.mult),
                          reads=[R_kT, R_einv], writes=[rkt])
                lastcol = 127 if d == 0 else 0
                P.dve(C("tensor_copy", out=et_, in_=ecum[:, :, lastcol]), reads=[R_ecum], writes=[ret])
                ps, rp = psum()
                P.pe(C("matmul", ps[:, :], lhsT=tri[:, 2 * d + 1, :], rhs=sp_, start=True, stop=True),
                     reads=[rsp, R_tri], writes=[rp])
                P.act(C("activation", out=erev, in_=ps[:, :], func=AF.Exp), reads=[rp], writes=[R_erev])
                P.dve(C("tensor_tensor", out=kh_, in0=ktm[:, ti, :], in1=erev, op=ALU.mult),
                      reads=[R_ktm, R_erev], writes=[rkh])

            def pool_a(ti, t, is_ctx):
                for half in range(2):
                    ps, rp = psum()
                    for k in range(4):
                        gi = half * 2 + k // 2
                        ch = k % 2
                        srcs = [(ti, poolL[:, gi, :], R_poolL)] if not is_ctx else \
                            [(tj, poolC[:, gi * 4 + tj * 2 + ti, :], R_poolC) for tj in range(2)]
                        for si, (tj, pm, rpm) in enumerate(srcs):
                            P.pe(C("matmul", ps[:, k * 128:(k + 1) * 128],
                                   lhsT=ub[:, tj, gi * 256 + ch * 128:gi * 256 + (ch + 1) * 128], rhs=pm,
                                   start=(si == 0), stop=(si == len(srcs) - 1)), reads=[R_ub, rpm], writes=[rp])
                    P.dve(C("tensor_copy", out=residT[:, half * 4:(half + 1) * 4, :].rearrange("p a b -> p (a b)"),
                            in_=ps[:, :]), reads=[rp], writes=[R_residT])

            def pool_b(ti, t):
                for half in range(2):
                    ps, rp = psum()
                    for k in range(4):
                        gi = half * 2 + k // 2
                        dh = k % 2
                        for ch in range(2):
                            P.pe(C("matmul", ps[:, k * 128:(k + 1) * 128],
                                   lhsT=wpool[:, gi * 2 + ch, dh * 128:(dh + 1) * 128], rhs=residT[:, gi * 2 + ch, :],
                                   start=(ch == 0), stop=(ch == 1)), reads=[R_wpool, R_residT], writes=[rp])
                    for k in range(4):
                        gi = half * 2 + k // 2
                        dh = k % 2
                        P.act(C("activation", out=mp[:, gi * 2 + dh, :], in_=ps[:, k * 128:(k + 1) * 128],
                                func=AF.Identity, scale=poolsT[:, gi * 2 + dh:gi * 2 + dh + 1]),
                              reads=[rp, R_poolsT], writes=[R_mp])
                P.dma("sync", st_mp[t, :, :], mp.rearrange("p a b -> p (a b)"), reads=[R_mp])

            def p1_block(blk, mode):
                full = (mode == "full")
                is_ctx = blk[0] < NCT
                v_ = 1 if is_ctx else 0
                nb = len(blk)
                N = nb * 128
                for ti, t in enumerate(blk):
                    xtile, rx = xt[cnt["x"] % 2]; cnt["x"] += 1
                    P.dma("sync", xtile, Xin[t * 128:(t + 1) * 128, :], writes=[rx])
                    ln_stats(xtile, rx)
                    normalize(xtile, rx, xtile, rx)
                    to_actT(xtile, rx, ti, v_, 0)
                yield
                for part in (range(2) if full else []):
                    wb, rw = wload(("fm", l, part), wview(w_fm[l])[:, :, part * 512:(part + 1) * 512])
                    for h in range(4):
                        ps, rp = psum()
                        for kc in range(16):
                            P.pe(C("matmul", ps[:, 0:N], lhsT=wb[:, kc, h * 128:(h + 1) * 128], rhs=actT[:, kc, 0:N],
                                   start=(kc == 0), stop=(kc == 15)), reads=[rw, R_actT], writes=[rp])
                        if part == 0:
                            P.act(C("activation", out=qT[:, h, 0:N], in_=ps[:, 0:N], func=AF.Identity, scale=QSCALE),
                                  reads=[rp], writes=[R_qT])
                        else:
                            P.dve(C("tensor_copy", out=kT[:, h, 0:N], in_=ps[:, 0:N]), reads=[rp], writes=[R_kT])
                wb, rw = wload(("fm", l, 2), wview(w_fm[l])[:, :, 1024:1088], ncol=64)
                for d in range(2):
                    ps, rp = psum()
                    for kc in range(16):
                        P.pe(C("matmul", ps[0:16, 0:N], lhsT=wb[:, kc, 32 * d:32 * d + 16], rhs=actT[:, kc, 0:N],
                               start=(kc == 0), stop=(kc == 15)), reads=[rw, R_actT], writes=[rp])
                    P.dve(C("tensor_copy", out=lrd[d][:, 0:N], in_=ps[0:16, 0:N]), reads=[rp], writes=[R_lrT])
                for c in range(7 if full else 3):
                    wb, rw = wload(("tm", l, c), wview(w_tm[l])[:, :, c * 512:(c + 1) * 512])
                    for ti, t in enumerate(blk):
                        ps, rp = psum()
                        for kc in range(16):
                            P.pe(C("matmul", ps[:, :], lhsT=actT[:, kc, ti * 128:(ti + 1) * 128], rhs=wb[:, kc, :],
                                   start=(kc == 0), stop=(kc == 15)), reads=[rw, R_actT], writes=[rp])
                        if c == 0:
                            P.dve(C("tensor_copy", out=ktm[:, ti, :], in_=ps[:, :]), reads=[rp], writes=[R_ktm])
                        elif c in (1, 2):
                            P.act(C("copy", out=vb[:, ti, (c - 1) * 512:c * 512], in_=ps[:, :]),
                                  reads=[rp], writes=[R_vb])
                        elif c in (3, 4):
                            sg_, rsg = sgr[cnt["sg"] % 2]; cnt["sg"] += 1
                            P.act(C("activation", out=sg_, in_=ps[:, :], func=AF.Silu), reads=[rp], writes=[rsg])
                            P.dma("sync", st_sg[t, :, (c - 3) * 512:(c - 2) * 512], sg_, reads=[rsg])
                        else:
                            P.dve(C("tensor_copy", out=ub[:, ti, (c - 5) * 512:(c - 4) * 512], in_=ps[:, :]),
                                  reads=[rp], writes=[R_ub])
                yield
                if not full:
                    if mode == "stateAB":
                        for ti in range(nb):
                            gates(ti, 0, False)
                            state_update(khd[0][0], etd[0][0], vb[:, ti, :], SA, SAb,
                                         [khd[0][1], etd[0][1], R_vb], R_SA, R_SAb)
                    for ti in range(nb - 1, -1, -1):
                        gates(ti, 1, False)
                        state_update(khd[1][0], etd[1][0], vb[:, ti, :], SB, SBb,
                                     [khd[1][1], etd[1][1], R_vb], R_SB, R_SBb)
                    return
                for ti, t in enumerate(blk):
                    for d in range(2):
                        gates_pre(ti, d)
                    pool_a(ti, t, is_ctx)
                    for d in range(2):
                        gates_post(ti, d, True)
                    pool_b(ti, t)
                    qt_, rqt = qtd[0]; kt_, rkt = ktd[0]; kh_, rkh = khd[0]; et_, ret = etd[0]
                    ob = scan_tile(qt_, kt_, kh_, et_, vb[:, ti, :], SA, SAb, PT, msk[:, 0, :],
                                   [rqt, rkt, rkh, ret, R_vb], R_SA, R_SAb, R_PT, R_msk)
                    for hp, (ps, rp) in enumerate(ob):
                        P.act(C("copy", out=oA[:, hp * 512:(hp + 1) * 512], in_=ps[:, :]), reads=[rp], writes=[R_oA])
                    P.dma("sync", st_oA[t, :, :], oA, reads=[R_oA])
                    qt_, rqt = qtd[1]; kt_, rkt = ktd[1]; kh_, rkh = khd[1]; et_, ret = etd[1]
                    P.dma("sync", st_qt[t, :, :], qt_.rearrange("p a b -> p (a b)"), reads=[rqt])
                    P.dma("sync", st_kt[t, :, :], kt_.rearrange("p a b -> p (a b)"), reads=[rkt])
                    P.dma("sync", st_kh[t, :, :], kh_, reads=[rkh])
                    P.dma("sync", st_et[t, :, :], et_, reads=[ret])
                    P.dma("sync", st_v[t, :, :], vb[:, ti, :], reads=[R_vb])

            if not last:
                p1_seq = [(blk, "full") for blk in blocks_ctx + blocks_lat]
                p2_blocks = blocks_ctx + blocks_lat[::-1]
                p3_blocks = blocks_ctx + blocks_lat
            else:
                p1_seq = [(blocks_ctx[0], "stateAB")] + [(blk, "full") for blk in blocks_own] + \
                         [(blk, "stateB") for blk in blocks_oth[::-1]]
                p2_blocks = blocks_own[::-1]
                p3_blocks = blocks_own
            gens = [p1_block(blk, mode) for blk, mode in p1_seq]
            next(gens[0])
            for gi_, g_ in enumerate(gens):
                next(g_)
                if gi_ + 1 < len(gens):
                    next(gens[gi_ + 1])
                for _ in g_:
                    pass
            P.barrier()

            areset(mark12)
            gbc, R_gbc = aalloc("gbc", [128, D]); lng, R_lng = aalloc("lng", [128, D]); lnb, R_lnb = aalloc("lnb", [128, D])
            junk_holder["junk"], junk_holder["rj"] = aalloc("junk2", [128, D], BF16)
            xblk, R_xblk = aalloc("xblk", [128, TB, D])
            R_xb = [Res(f"xb{i}") for i in range(TB)]
            tmp5, R_tmp5 = aalloc("tmp5", [128, 512])
            ldA = []
            for i in range(3):
                ldA.append(dict(
                    qt=aalloc(f"l_qt{i}", [128, 4, 128], BF16), kt=aalloc(f"l_kt{i}", [128, 4, 128], BF16),
                    kh=aalloc(f"l_kh{i}", [128, 512], BF16), et=aalloc(f"l_et{i}", [128, 4]),
                    v=aalloc(f"l_v{i}", [128, 1024], BF16)))
            ldB = []
            for i in range(2):
                ldB.append(dict(oA=aalloc(f"l_oA{i}", [128, 1024]), sg=aalloc(f"l_sg{i}", [128, 1024])))
            PT, R_PT = aalloc("PT2", [128, 4, 128], BF16)
            P.dma("sync", lng, ln1_g[l:l + 1, :].partition_broadcast(128), writes=[R_lng])
            P.dma("sync", lnb, ln1_b[l:l + 1, :].partition_broadcast(128), writes=[R_lnb])
            cntA = {"a": 0, "b": 0}

            def p2_loadA(t):
                a_ = ldA[cntA["a"] % 3]; cntA["a"] += 1
                P.dma("sync", a_["qt"][0].rearrange("p a b -> p (a b)"), st_qt[t, :, :], writes=[a_["qt"][1]])
                P.dma("sync", a_["kt"][0].rearrange("p a b -> p (a b)"), st_kt[t, :, :], writes=[a_["kt"][1]])
                P.dma("sync", a_["kh"][0], st_kh[t, :, :], writes=[a_["kh"][1]])
                P.dma("sync", a_["et"][0], st_et[t, :, :], writes=[a_["et"][1]])
                P.dma("sync", a_["v"][0], st_v[t, :, :], writes=[a_["v"][1]])
                return a_

            def p2_front(ti, t, a_):
                b_ = ldB[cntA["b"] % 2]; cntA["b"] += 1
                P.dma("sync", b_["oA"][0], st_oA[t, :, :], writes=[b_["oA"][1]])
                P.dma("sync", b_["sg"][0], st_sg[t, :, :], writes=[b_["sg"][1]])
                ob = scan_tile(a_["qt"][0], a_["kt"][0], a_["kh"][0], a_["et"][0], a_["v"][0], SB, SBb, PT,
                               msk[:, 1, :], [a_["qt"][1], a_["kt"][1], a_["kh"][1], a_["et"][1], a_["v"][1]],
                               R_SB, R_SBb, R_PT, R_msk, ofix=(4, 5) if (cntA["b"] % 2) else (6, 7))
                return b_, ob

            def p2_back(ti, t, b_, ob):
                P.dma("sync", actT[:, 8:16, ti * 128:(ti + 1) * 128],
                      st_mp[t, :, :].rearrange("p (a b) -> p a b", b=128), writes=[R_actT])
                o_, ro = b_["oA"]
                sg_, rsg = b_["sg"]
                for hp, (ps, rp) in enumerate(ob):
                    P.dve(C("tensor_tensor", out=o_[:, hp * 512:(hp + 1) * 512], in0=o_[:, hp * 512:(hp + 1) * 512],
                            in1=ps[:, :], op=ALU.add), reads=[rp, ro], writes=[ro])
                jk, rj = junk_holder["junk"], junk_holder["rj"]
                stt, R_stt = next_stats()
                P.dve(C("memset", stt[:, 4:8], 0.0), writes=[R_stt])
                for h in range(4):
                    P.act(C("activation", out=jk[:, h * 256:(h + 1) * 256], in_=o_[:, h * 256:(h + 1) * 256],
                            func=AF.Square, accum_out=stt[:, 4 + h:5 + h]), reads=[ro], writes=[rj, R_stt])
                P.dve(C("tensor_scalar", out=stt[:, 4:8], in0=stt[:, 4:8], scalar1=1.0 / 256, scalar2=EPS,
                        op0=ALU.mult, op1=ALU.add), reads=[R_stt], writes=[R_stt])
                P.act(C("activation", out=stt[:, 4:8], in_=stt[:, 4:8], func=AF.Ln), reads=[R_stt], writes=[R_stt])
                P.act(C("activation", out=stt[:, 4:8], in_=stt[:, 4:8], func=AF.Exp, scale=-0.5),
                      reads=[R_stt], writes=[R_stt])
                for h in range(4):
                    P.dve(C("scalar_tensor_tensor", out=o_[:, h * 256:(h + 1) * 256],
                            in0=o_[:, h * 256:(h + 1) * 256], scalar=stt[:, 4 + h:5 + h], in1=normgb,
                            op0=ALU.mult, op1=ALU.mult), reads=[ro, R_stt, R_normgb], writes=[ro])
                P.dve(C("tensor_tensor", out=o_, in0=o_, in1=sg_, op=ALU.mult), reads=[ro, rsg], writes=[ro])
                for half in range(2):
                    ps, rp = psum()
                    for k in range(4):
                        cc = half * 4 + k
                        P.pe(C("transpose", out=ps[:, k * 128:(k + 1) * 128], in_=o_[:, cc * 128:(cc + 1) * 128],
                               identity=ident[:]), reads=[ro, R_ident], writes=[rp])
                    for k in range(4):
                        cc = half * 4 + k
                        P.act(C("copy", out=actT[:, cc, ti * 128:(ti + 1) * 128], in_=ps[:, k * 128:(k + 1) * 128]),
                              reads=[rp], writes=[R_actT])

            cur_v = None
            state["ps_pool"] = [0, 1, 2, 3]
            for blk in p2_blocks:
                is_ctx = blk[0] < NCT
                v_ = 1 if is_ctx else 0
                nb = len(blk)
                order = list(range(nb - 1, -1, -1))
                nxtA = p2_loadA(blk[order[0]])
                if cur_v != v_:
                    P.dma("sync", gbc, modrow[l, v_:v_ + 1, 2 * D:3 * D].partition_broadcast(128), writes=[R_gbc])
                    cur_v = v_
                for ti, t in enumerate(blk):
                    P.dma("sync", xblk[:, ti, :], Xin[t * 128:(t + 1) * 128, :], writes=[R_xb[ti]])
                prev = None
                for oi, ti in enumerate(order):
                    curA = nxtA
                    if oi + 1 < nb:
                        nxtA = p2_loadA(blk[order[oi + 1]])
                    b_, ob = p2_front(ti, blk[ti], curA)
                    if prev is not None:
                        p2_back(*prev)
                    prev = (ti, blk[ti], b_, ob)
                p2_back(*prev)
                for c in range(4):
                    wb, rw = wload(("out", l, c), wview(w_out[l])[:, :, c * 512:(c + 1) * 512])
                    for ti, t in enumerate(blk):
                        ps, rp = psum()
                        for kc in range(16):
                            P.pe(C("matmul", ps[:, :], lhsT=actT[:, kc, ti * 128:(ti + 1) * 128], rhs=wb[:, kc, :],
                                   start=(kc == 0), stop=(kc == 15)), reads=[rw, R_actT], writes=[rp])
                        P.dve(C("tensor_tensor", out=tmp5, in0=ps[:, :], in1=gbc[:, c * 512:(c + 1) * 512], op=ALU.mult),
                              reads=[rp, R_gbc], writes=[R_tmp5])
                        P.dve(C("scalar_tensor_tensor", out=xblk[:, ti, c * 512:(c + 1) * 512],
                                in0=xblk[:, ti, c * 512:(c + 1) * 512], scalar=ALPHA, in1=tmp5,
                                op0=ALU.mult, op1=ALU.add), reads=[R_xb[ti], R_tmp5], writes=[R_xb[ti]])
                for ti, t in enumerate(blk):
                    xa = xblk[:, ti, :]
                    ln_stats(xa, R_xb[ti], act_sum=True)
                    ln_affine(xa, R_xb[ti], lng, R_lng, lnb, R_lnb)
                    P.dma("sync", X1s[t * 128:(t + 1) * 128, :], xa, reads=[R_xb[ti]])
            P.barrier()

            state["ps_pool"] = list(range(8))
            areset()
            gbc, R_gbc = aalloc("gbc3", [128, D]); lng, R_lng = aalloc("lng3", [128, D]); lnb, R_lnb = aalloc("lnb3", [128, D])
            xn3, R_xn3 = aalloc("xn3", [128, D])
            junk_holder["junk"], junk_holder["rj"] = xn3, R_xn3
            xblk, R_xblk = aalloc("xblk3", [128, TB, D])
            R_xb = [Res(f"xb3{i}") for i in range(TB)]
            hidT, R_hidT = aalloc("hidT", [128, 64, 512], BF16)
            tmpr = [aalloc(f"tmpr{i}", [128, 512]) for i in range(2)]
            tmp5, R_tmp5 = aalloc("tmp53", [128, 512])
            P.dma("sync", lng, ln2_g[l:l + 1, :].partition_broadcast(128), writes=[R_lng])
            P.dma("sync", lnb, ln2_b[l:l + 1, :].partition_broadcast(128), writes=[R_lnb])
            rc = {"n": 0}

            def p3_prologue(blk):
                v_ = 1 if blk[0] < NCT else 0
                for ti, t in enumerate(blk):
                    P.dma("sync", xn3, X1s[t * 128:(t + 1) * 128, :], writes=[R_xn3])
                    ln_stats(xn3, R_xn3, junk=xblk[:, 0, :], rj=R_xb[0])
                    normalize(xn3, R_xn3, xn3, R_xn3)
                    to_actT(xn3, R_xn3, ti, v_, 1)

            def p3_phaseA(blk):
                N = len(blk) * 128
                for j in range(16):
                    wb, rw = wload(("w1", l, j), wview(w1[l])[:, :, j * 512:(j + 1) * 512])
                    for fs in range(4):
                        f = j * 4 + fs
                        ps, rp = psum()
                        for kc in range(16):
                            P.pe(C("matmul", ps[:, 0:N], lhsT=wb[:, kc, fs * 128:(fs + 1) * 128], rhs=actT[:, kc, 0:N],
                                   start=(kc == 0), stop=(kc == 15)), reads=[rw, R_actT], writes=[rp])
                        tr_, rtr = tmpr[rc["n"] % 2]; rc["n"] += 1
                        P.act(C("activation", out=tr_[:, 0:N], in_=ps[:, 0:N], func=AF.Relu, bias=b1T[:, f:f + 1],
                                scale=1.0), reads=[rp, R_b1T], writes=[rtr])
                        P.dve(C("tensor_tensor", out=hidT[:, f, 0:N], in0=tr_[:, 0:N], in1=tr_[:, 0:N], op=ALU.mult),
                              reads=[rtr], writes=[R_hidT])

            def p3_phaseB(blk):
                nb = len(blk)
                for ti, t in enumerate(blk):
                    P.dma("sync", xblk[:, ti, :], X1s[t * 128:(t + 1) * 128, :], writes=[R_xb[ti]])
                for c in range(4):
                    acc = [psum() for _ in range(nb)]
                    for ti in range(nb):
                        ps, rp = acc[ti]
                        P.pe(C("matmul", ps[:, :], lhsT=oneh[0:4, c, :], rhs=b2r[0:4, :], start=True, stop=False),
                             reads=[R_oneh, R_b2r], writes=[rp])
                    for g in range(4):
                        wb, rw = wload(("w2", l, c, g),
                                       w2[l].rearrange("(g fi p) n -> g p fi n", fi=16, p=128)[g][:, :, c * 512:(c + 1) * 512])
                        for fi in range(16):
                            f = g * 16 + fi
                            for ti in range(nb):
                                ps, rp = acc[ti]
                                P.pe(C("matmul", ps[:, :], lhsT=hidT[:, f, ti * 128:(ti + 1) * 128], rhs=wb[:, fi, :],
                                       start=False, stop=(g == 3 and fi == 15)), reads=[rw, R_hidT], writes=[rp])
                    for ti in range(nb):
                        ps, rp = acc[ti]
                        P.dve(C("tensor_tensor", out=tmp5, in0=ps[:, :], in1=gbc[:, c * 512:(c + 1) * 512], op=ALU.mult),
                              reads=[rp, R_gbc], writes=[R_tmp5])
                        P.dve(C("scalar_tensor_tensor", out=xblk[:, ti, c * 512:(c + 1) * 512],
                                in0=xblk[:, ti, c * 512:(c + 1) * 512], scalar=ALPHA, in1=tmp5,
                                op0=ALU.mult, op1=ALU.add), reads=[R_xb[ti], R_tmp5], writes=[R_xb[ti]])

            def p3_final(blk):
                for ti, t in enumerate(blk):
                    xa = xblk[:, ti, :]
                    ln_stats(xa, R_xb[ti], junk=xn3, rj=R_xn3, act_sum=True)
                    ln_affine(xa, R_xb[ti], lng, R_lng, lnb, R_lnb)
                    if last:
                        P.dma("sync", y[(t - NCT) * 128:(t - NCT + 1) * 128, :], xa, reads=[R_xb[ti]])
                    else:
                        P.dma("sync", X2s[t * 128:(t + 1) * 128, :], xa, reads=[R_xb[ti]])

            cur_v = None
            p3_prologue(p3_blocks[0])
            for bi_, blk in enumerate(p3_blocks):
                v_ = 1 if blk[0] < NCT else 0
                p3_phaseA(blk)
                if bi_ + 1 < len(p3_blocks):
                    p3_prologue(p3_blocks[bi_ + 1])
                if cur_v != v_:
                    P.dma("sync", gbc, modrow[l, v_:v_ + 1, 5 * D:6 * D].partition_broadcast(128), writes=[R_gbc])
                    cur_v = v_
                p3_phaseB(blk)
                p3_final(blk)
            P.barrier()
        P.emit(st)
    return nc


def _pool_mats():
    wins = (2, 4, 8, 16)

    def A(seg_len, w):
        pos = np.arange(seg_len)
        lo = np.maximum(pos - w // 2, 0)
        hi = np.minimum(pos + w // 2 - 1, seg_len - 1)
        M = np.zeros((seg_len, seg_len), np.float32)
        for i in range(seg_len):
            M[i, lo[i]:hi[i] + 1] = 1.0 / float(hi[i] - lo[i] + 1)
            M[i, i] -= 1.0
        return M
    poolL = np.zeros((128, 4, 128), np.float32)
    poolC = np.zeros((128, 16, 128), np.float32)
    for gi, w in enumerate(wins):
        a64 = A(64, w)
        full = np.zeros((128, 128), np.float32)
        full[:64, :64] = a64
        full[64:, 64:] = a64
        poolL[:, gi, :] = full.T
        a256 = A(256, w)
        for tj in range(2):
            for ti in range(2):
                poolC[:, gi * 4 + tj * 2 + ti, :] = a256[ti * 128:(ti + 1) * 128, tj * 128:(tj + 1) * 128].T
    return poolL, poolC


def _consts():
    j = np.arange(128)[:, None]
    i = np.arange(128)[None, :]
    s = np.float32(-1.0 / 16.0)
    tri = np.zeros((128, 4, 128), np.float32)
    tri[:, 0, :] = (j <= i) * s
    tri[:, 1, :] = (j > i) * s
    tri[:, 2, :] = (j >= i) * s
    tri[:, 3, :] = (j < i) * s
    mask = np.zeros((128, 2, 128), np.float32)
    mask[:, 0, :] = (j <= i)
    mask[:, 1, :] = (j >= i)
    oneh = np.zeros((4, 4, 128), np.float32)
    for k in range(4):
        oneh[k, k, :] = 1.0
    poolL, poolC = _pool_mats()
    return dict(c_ident=np.eye(128, dtype=np.float32), c_tri=tri, c_mask=mask, c_poolL=poolL, c_poolC=poolC,
                c_oneh=oneh)


_NC_CACHE = {}


def kernel(x, c, ctx, c_ctx, w_ada, b_ada, w_in, w_gate_up, b_gate, gla_norm_g, w_pool, pool_scale, w_out,
           ln1_g, ln1_b, w_mlp1, b_mlp1, w_mlp2, b_mlp2, ln2_g, ln2_b):
    f = lambda a: np.ascontiguousarray(np.asarray(a, dtype=np.float32))
    x = f(x); c = f(c); ctx = f(ctx); c_ctx = f(c_ctx); w_in = f(w_in)
    z16 = np.zeros((L, D, 16), np.float32)
    kc_, vc_, lrf, lrb = w_in[:, :, 0:512], w_in[:, :, 512:1536], w_in[:, :, 1536:1552], w_in[:, :, 1552:1568]
    q_, g_, u_ = w_in[:, :, 1568:2080], w_in[:, :, 2080:3104], w_in[:, :, 3104:4128]
    w_tm = np.ascontiguousarray(np.concatenate([kc_, vc_, g_, u_], axis=2))
    shared = dict(
        w_ada=f(w_ada), b_ada=f(b_ada), w_tm=w_tm, normg=f(gla_norm_g), w_pool=f(w_pool), pool_scale=f(pool_scale),
        w_out=f(w_out), ln1_g=f(ln1_g), ln1_b=f(ln1_b), w1=f(w_mlp1), b1=f(b_mlp1), w2=f(w_mlp2), b2=f(b_mlp2),
        ln2_g=f(ln2_g), ln2_b=f(ln2_b))
    in_maps = []
    T = NLT * 128
    consts = _consts()
    wg = f(w_gate_up); bg = f(b_gate)
    per_par = []
    for par in range(2):
        lrA, lrB = (lrf, lrb) if par == 0 else (lrb, lrf)
        d = dict(shared)
        d["w_fm"] = np.ascontiguousarray(np.concatenate([q_, kc_, lrA, z16, lrB, z16], axis=2))
        d["w_gate"] = np.ascontiguousarray(wg if par == 0 else wg[:, ::-1])
        d["b_gate"] = np.ascontiguousarray((bg if par == 0 else bg[:, ::-1]).reshape(L, 1, 1024))
        d.update(consts)
        if par == 1:
            d["c_poolL"] = np.ascontiguousarray(consts["c_poolL"][::-1, :, ::-1])
            pc = consts["c_poolC"].reshape(128, 4, 2, 2, 128)
            d["c_poolC"] = np.ascontiguousarray(pc[::-1, :, ::-1, ::-1, ::-1].reshape(128, 16, 128))
        per_par.append(d)
    for core in range(N_CORES):
        b = core // 2 if SPLIT else core % 4
        par = core % 2 if SPLIT else 0
        m = dict(per_par[par])
        xb = x[b][:T]
        cb = ctx[b]
        if par == 1:
            xb = xb[::-1]
            cb = cb[::-1]
        m["xin"] = np.ascontiguousarray(np.concatenate([cb, xb], axis=0))
        m["cvec"] = np.ascontiguousarray(np.stack([c[b], c_ctx], axis=0).reshape(32, 128))
        in_maps.append(m)
    if "nc" not in _NC_CACHE:
        _NC_CACHE["nc"] = build_nc()
    res = run_bass_kernel_spmd(_NC_CACHE["nc"], in_maps, core_ids=list(range(N_CORES)))
    kernel.last_results = res.results
    nb_out = N_CORES // 2 if SPLIT else min(4, N_CORES)
    out = np.zeros((nb_out, T, D), np.float32)
    if SPLIT:
        for core in range(N_CORES):
            yb = res.results[core]["y"].reshape(T // 2, D)
            if core % 2 == 0:
                out[core // 2, 0:T // 2] = yb
            else:
                out[core // 2, T // 2:T] = yb[::-1]
    else:
        for b in range(nb_out):
            out[b] = res.results[b]["y"].reshape(T, D)
    return out.astype(np.float32)
```
